# Optimizing a Trainium2 kernel written in Bass

```python
import numpy as np
import jax
import jax.numpy as jnp
from jax import lax

D_MODEL = 1024
BATCH = 8
SEQ = 2048
DEPTH = 4

N_MIXERS = 3
HEAD_DIM = 64
N_HEADS = D_MODEL // HEAD_DIM
D_FF = 2816
RMS_EPS = 1e-6
ROPE_THETA = 10000.0
BLOCK_Q = 128
NEG = -1e30

NSA_GROUPS = 4
NSA_REP = N_HEADS // NSA_GROUPS
NSA_CMP_LEN = 32
NSA_CMP_STRIDE = 16
NSA_CMP_HIDDEN = 2 * HEAD_DIM
NSA_SLC_LEN = 64
NSA_TOPK = 16
NSA_WINDOW = 512
NSA_SLC_QBLOCK = 32
NSA_FORCE_BONUS = 1e4

SWA_KV_HEADS = 2
SWA_REP = N_HEADS // SWA_KV_HEADS
SWA_WINDOW = 128

FOX_HEADS = N_HEADS

N_A = len(range(0, DEPTH, N_MIXERS))
N_B = len(range(1, DEPTH, N_MIXERS))
N_C = len(range(2, DEPTH, N_MIXERS))

Q_WIDTH = N_HEADS * HEAD_DIM
NSA_KV_WIDTH = 3 * 2 * NSA_GROUPS * HEAD_DIM
NSA_IN = Q_WIDTH + NSA_KV_WIDTH + 3 * N_HEADS
SWA_IN = Q_WIDTH + 2 * SWA_KV_HEADS * HEAD_DIM
FOX_IN = 3 * FOX_HEADS * HEAD_DIM + FOX_HEADS

kernel_name = 'hybrid_nsa_swa_fox_macaron'


def rmsnorm(x, g):
    xf = x.astype(jnp.float32)
    y = xf * lax.rsqrt(jnp.mean(xf * xf, axis=-1, keepdims=True) + RMS_EPS)
    return (y * g.astype(jnp.float32)).astype(x.dtype)


def swiglu(x, w_gu, w_down):
    g, u = jnp.split(x @ w_gu, 2, axis=-1)
    return (jax.nn.silu(g) * u) @ w_down


def rope_tables(S):
    inv = ROPE_THETA ** (-jnp.arange(0, HEAD_DIM, 2, dtype=jnp.float32) / HEAD_DIM)
    ang = jnp.arange(S, dtype=jnp.float32)[:, None] * inv[None, :]
    return jnp.cos(ang), jnp.sin(ang)


def apply_rope(x, cos, sin):
    x1, x2 = jnp.split(x.astype(jnp.float32), 2, axis=-1)
    c = cos[None, :, None, :]
    s = sin[None, :, None, :]
    return jnp.concatenate([x1 * c - x2 * s, x2 * c + x1 * s], axis=-1).astype(x.dtype)


def banded_attention(q, k, v, window, sinks=None):
    B, S, G, R, dh = q.shape
    span = window + BLOCK_Q
    kp = jnp.pad(k, ((0, 0), (window, 0), (0, 0), (0, 0)))
    vp = jnp.pad(v, ((0, 0), (window, 0), (0, 0), (0, 0)))
    scale = dh ** -0.5

    def step(i):
        start = i * BLOCK_Q
        qi = lax.dynamic_slice_in_dim(q, start, BLOCK_Q, axis=1)
        ki = lax.dynamic_slice_in_dim(kp, start, span, axis=1)
        vi = lax.dynamic_slice_in_dim(vp, start, span, axis=1)
        s = jnp.einsum('bqgrd,bkgd->bgrqk', qi, ki).astype(jnp.float32) * scale
        qpos = start + jnp.arange(BLOCK_Q)
        kpos = start - window + jnp.arange(span)
        diff = qpos[:, None] - kpos[None, :]
        mask = (diff >= 0) & (diff < window) & (kpos[None, :] >= 0)
        s = jnp.where(mask, s, NEG)
        if sinks is None:
            p = jax.nn.softmax(s, axis=-1)
        else:
            sk = sinks.astype(jnp.float32)[None, :, :, None, None]
            m = jnp.maximum(jnp.max(s, axis=-1, keepdims=True), sk)
            e = jnp.exp(s - m)
            p = e / (jnp.sum(e, axis=-1, keepdims=True) + jnp.exp(sk - m))
        return jnp.einsum('bgrqk,bkgd->bqgrd', p.astype(vi.dtype), vi)

    o = lax.map(step, jnp.arange(S // BLOCK_Q))
    return jnp.moveaxis(o, 0, 1).reshape(B, S, G, R, dh)


def compress_blocks(x, blk_idx, pe, w1, w2):
    B, _, G, dh = x.shape
    n_cmp, L = blk_idx.shape
    xb = x[:, blk_idx] + pe[:, None, :]
    xb = xb.transpose(0, 1, 3, 2, 4).reshape(B, n_cmp, G, L * dh)
    return jax.nn.gelu(xb @ w1) @ w2


def cmp_to_slc_overlap(n_cmp, n_slc):
    cs = np.arange(n_cmp) * NSA_CMP_STRIDE
    ce = cs + NSA_CMP_LEN
    ss = np.arange(n_slc) * NSA_SLC_LEN
    se = ss + NSA_SLC_LEN
    ov = np.clip(np.minimum(ce[:, None], se[None, :]) - np.maximum(cs[:, None], ss[None, :]), 0, None)
    return jnp.asarray(ov / NSA_CMP_LEN, dtype=jnp.float32)


def selected_block_attention(q, k, v, sel):
    B, S, G, R, dh = q.shape
    n_sel = sel.shape[-1]
    n_slc = S // NSA_SLC_LEN
    m = n_sel * NSA_SLC_LEN
    kblk = k.reshape(B, n_slc, NSA_SLC_LEN, G, dh).transpose(0, 3, 1, 2, 4)
    vblk = v.reshape(B, n_slc, NSA_SLC_LEN, G, dh).transpose(0, 3, 1, 2, 4)
    gather = jax.vmap(jax.vmap(lambda blocks, ids: blocks[ids]))
    offs = jnp.arange(NSA_SLC_LEN)
    scale = dh ** -0.5

    def step(i):
        start = i * NSA_SLC_QBLOCK
        qi = lax.dynamic_slice_in_dim(q, start, NSA_SLC_QBLOCK, axis=1)
        ids = lax.dynamic_slice_in_dim(sel, start, NSA_SLC_QBLOCK, axis=2)
        kg = gather(kblk, ids).reshape(B, G, NSA_SLC_QBLOCK, m, dh)
        vg = gather(vblk, ids).reshape(B, G, NSA_SLC_QBLOCK, m, dh)
        kpos = (ids[..., None] * NSA_SLC_LEN + offs).reshape(B, G, NSA_SLC_QBLOCK, m)
        qpos = start + jnp.arange(NSA_SLC_QBLOCK)
        mask = (kpos <= qpos[:, None])[:, :, None]
        s = jnp.einsum('bqgrd,bgqmd->bgrqm', qi, kg).astype(jnp.float32) * scale
        p = jax.nn.softmax(jnp.where(mask, s, NEG), axis=-1)
        return jnp.einsum('bgrqm,bgqmd->bqgrd', p.astype(vg.dtype), vg)

    o = lax.map(step, jnp.arange(S // NSA_SLC_QBLOCK))
    return jnp.moveaxis(o, 0, 1).reshape(B, S, G, R, dh)


def nsa_mixer(h, cos, sin, w_in, ck_pe, ck_w1, ck_w2, cv_pe, cv_w1, cv_w2, w_out):
    B, S, _ = h.shape
    G, R, dh = NSA_GROUPS, NSA_REP, HEAD_DIM
    scale = dh ** -0.5
    proj = h @ w_in
    q = apply_rope(proj[..., :Q_WIDTH].reshape(B, S, N_HEADS, dh), cos, sin).reshape(B, S, G, R, dh)
    kv = proj[..., Q_WIDTH:Q_WIDTH + NSA_KV_WIDTH].reshape(B, S, 3, 2, G, dh)
    gates = jax.nn.sigmoid(proj[..., Q_WIDTH + NSA_KV_WIDTH:].astype(jnp.float32)).reshape(B, S, 3, G, R, 1)
    k = apply_rope(kv[:, :, :, 0].reshape(B, S, 3 * G, dh), cos, sin).reshape(B, S, 3, G, dh)
    v = kv[:, :, :, 1]
    t = jnp.arange(S)

    n_cmp = (S - NSA_CMP_LEN) // NSA_CMP_STRIDE + 1
    blk_idx = jnp.arange(n_cmp)[:, None] * NSA_CMP_STRIDE + jnp.arange(NSA_CMP_LEN)[None, :]
    k_cmp = compress_blocks(k[:, :, 0], blk_idx, ck_pe, ck_w1, ck_w2)
    v_cmp = compress_blocks(v[:, :, 0], blk_idx, cv_pe, cv_w1, cv_w2)
    s_cmp = jnp.einsum('bsgrd,bngd->bgrsn', q, k_cmp).astype(jnp.float32) * scale
    cmp_valid = (jnp.arange(n_cmp) * NSA_CMP_STRIDE + NSA_CMP_LEN - 1)[None, :] <= t[:, None]
    p_cmp = jax.nn.softmax(jnp.where(cmp_valid, s_cmp, NEG), axis=-1) * cmp_valid
    o_cmp = jnp.einsum('bgrsn,bngd->bsgrd', p_cmp.astype(v_cmp.dtype), v_cmp)

    n_slc = S // NSA_SLC_LEN
    k_sel = min(NSA_TOPK, n_slc)
    imp = jnp.einsum('bgrsn,nj->bgsj', p_cmp, cmp_to_slc_overlap(n_cmp, n_slc))
    tb = (t // NSA_SLC_LEN)[:, None]
    j = jnp.arange(n_slc)[None, :]
    forced = (j == 0) | (j == tb) | (j == tb - 1)
    imp = jnp.where(j > tb, NEG, jnp.where(forced, NSA_FORCE_BONUS, imp))
    _, sel = lax.top_k(imp, k_sel)
    o_slc = selected_block_attention(q, k[:, :, 1], v[:, :, 1], sel)

    o_win = banded_attention(q, k[:, :, 2], v[:, :, 2], NSA_WINDOW)

    o = gates[:, :, 0] * o_cmp + gates[:, :, 1] * o_slc + gates[:, :, 2] * o_win
    return o.astype(h.dtype).reshape(B, S, Q_WIDTH) @ w_out


def swa_mixer(h, cos, sin, w_in, sinks, w_out):
    B, S, _ = h.shape
    proj = h @ w_in
    q = apply_rope(proj[..., :Q_WIDTH].reshape(B, S, N_HEADS, HEAD_DIM), cos, sin)
    q = q.reshape(B, S, SWA_KV_HEADS, SWA_REP, HEAD_DIM)
    kv = proj[..., Q_WIDTH:].reshape(B, S, 2, SWA_KV_HEADS, HEAD_DIM)
    k = apply_rope(kv[:, :, 0], cos, sin)
    v = kv[:, :, 1]
    o = banded_attention(q, k, v, SWA_WINDOW, sinks.reshape(SWA_KV_HEADS, SWA_REP))
    return o.reshape(B, S, Q_WIDTH) @ w_out


def fox_mixer(h, w_in, b_f, w_out):
    B, S, _ = h.shape
    dh = HEAD_DIM
    scale = dh ** -0.5
    proj = h @ w_in
    qkv = proj[..., :3 * FOX_HEADS * dh].reshape(B, S, 3, FOX_HEADS, dh)
    q, k, v = qkv[:, :, 0], qkv[:, :, 1], qkv[:, :, 2]
    log_f = jax.nn.log_sigmoid((proj[..., 3 * FOX_HEADS * dh:] + b_f).astype(jnp.float32))
    c = jnp.cumsum(log_f, axis=1).transpose(0, 2, 1)
    kpos = jnp.arange(S)

    def step(i):
        start = i * BLOCK_Q
        qi = lax.dynamic_slice_in_dim(q, start, BLOCK_Q, axis=1)
        ci = lax.dynamic_slice_in_dim(c, start, BLOCK_Q, axis=2)
        s = jnp.einsum('bqhd,bkhd->bhqk', qi, k).astype(jnp.float32) * scale
        s = s + ci[..., None] - c[:, :, None, :]
        qpos = start + jnp.arange(BLOCK_Q)
        p = jax.nn.softmax(jnp.where(kpos[None, :] <= qpos[:, None], s, NEG), axis=-1)
        return jnp.einsum('bhqk,bkhd->bqhd', p.astype(v.dtype), v)

    o = lax.map(step, jnp.arange(S // BLOCK_Q))
    return jnp.moveaxis(o, 0, 1).reshape(B, S, FOX_HEADS * dh) @ w_out


def setup_inputs(seed: int = 0) -> dict:
    key = jax.random.key(seed)
    ks = iter(jax.random.split(key, 32))
    f32 = jnp.float32

    def nrm(shape, scale):
        return scale * jax.random.normal(next(ks), shape, f32)

    def gain(shape):
        return 1.0 + 0.01 * jax.random.normal(next(ks), shape, f32)

    L, dh, Hc = NSA_CMP_LEN, HEAD_DIM, NSA_CMP_HIDDEN
    return {
        'x': nrm((BATCH, SEQ, D_MODEL), 1.0),
        'ffn1_norm': gain((DEPTH, D_MODEL)),
        'ffn1_w_gu': nrm((DEPTH, D_MODEL, 2 * D_FF), D_MODEL ** -0.5),
        'ffn1_w_down': nrm((DEPTH, D_FF, D_MODEL), D_FF ** -0.5),
        'mix_norm': gain((DEPTH, D_MODEL)),
        'ffn2_norm': gain((DEPTH, D_MODEL)),
        'ffn2_w_gu': nrm((DEPTH, D_MODEL, 2 * D_FF), D_MODEL ** -0.5),
        'ffn2_w_down': nrm((DEPTH, D_FF, D_MODEL), D_FF ** -0.5),
        'nsa_w_in': nrm((N_A, D_MODEL, NSA_IN), D_MODEL ** -0.5),
        'nsa_ck_pe': nrm((N_A, L, dh), 0.1),
        'nsa_ck_w1': nrm((N_A, L * dh, Hc), (L * dh) ** -0.5),
        'nsa_ck_w2': nrm((N_A, Hc, dh), Hc ** -0.5),
        'nsa_cv_pe': nrm((N_A, L, dh), 0.1),
        'nsa_cv_w1': nrm((N_A, L * dh, Hc), (L * dh) ** -0.5),
        'nsa_cv_w2': nrm((N_A, Hc, dh), Hc ** -0.5),
        'nsa_w_out': nrm((N_A, Q_WIDTH, D_MODEL), Q_WIDTH ** -0.5),
        'swa_w_in': nrm((N_B, D_MODEL, SWA_IN), D_MODEL ** -0.5),
        'swa_sinks': nrm((N_B, N_HEADS), 1.0),
        'swa_w_out': nrm((N_B, Q_WIDTH, D_MODEL), Q_WIDTH ** -0.5),
        'fox_w_in': nrm((N_C, D_MODEL, FOX_IN), D_MODEL ** -0.5),
        'fox_b_f': jax.random.uniform(next(ks), (N_C, FOX_HEADS), f32, 1.0, 4.0),
        'fox_w_out': nrm((N_C, FOX_HEADS * HEAD_DIM, D_MODEL), (FOX_HEADS * HEAD_DIM) ** -0.5),
        'final_norm': gain((D_MODEL,)),
    }


def reference(x, ffn1_norm, ffn1_w_gu, ffn1_w_down, mix_norm, ffn2_norm, ffn2_w_gu, ffn2_w_down,
              nsa_w_in, nsa_ck_pe, nsa_ck_w1, nsa_ck_w2, nsa_cv_pe, nsa_cv_w1, nsa_cv_w2, nsa_w_out,
              swa_w_in, swa_sinks, swa_w_out, fox_w_in, fox_b_f, fox_w_out, final_norm):
    S = x.shape[1]
    cos, sin = rope_tables(S)
    for i in range(DEPTH):
        kind, j = i % N_MIXERS, i // N_MIXERS
        x = x + 0.5 * swiglu(rmsnorm(x, ffn1_norm[i]), ffn1_w_gu[i], ffn1_w_down[i])
        h = rmsnorm(x, mix_norm[i])
        if kind == 0:
            y = nsa_mixer(h, cos, sin, nsa_w_in[j], nsa_ck_pe[j], nsa_ck_w1[j], nsa_ck_w2[j],
                          nsa_cv_pe[j], nsa_cv_w1[j], nsa_cv_w2[j], nsa_w_out[j])
        elif kind == 1:
            y = swa_mixer(h, cos, sin, swa_w_in[j], swa_sinks[j], swa_w_out[j])
        else:
            y = fox_mixer(h, fox_w_in[j], fox_b_f[j], fox_w_out[j])
        x = x + y
        x = x + 0.5 * swiglu(rmsnorm(x, ffn2_norm[i]), ffn2_w_gu[i], ffn2_w_down[i])
    return rmsnorm(x, final_norm)
```

```python
import numpy as np
import concourse.bass as bass
import concourse.mybir as mybir
from concourse.bass_utils import run_bass_kernel_spmd
from contextlib import ExitStack

F32 = mybir.dt.float32
BF16 = mybir.dt.bfloat16
AF = mybir.ActivationFunctionType
ALU = mybir.AluOpType

S = 2048
D = 1024
DEPTH = 4
DFF = 2816
NT = 16
KC = 8
NB = 4
NJ = 22
EPS = 1e-6
NEGM = -30000.0

ENGS = ("pe", "act", "dve", "pool", "sp")


class Prog:
    def __init__(self, nc):
        self.nc = nc
        self.ops = []
        self.es = ExitStack()

    def sbuf(self, name, shape, dt):
        return self.es.enter_context(self.nc.sbuf_tensor(name, list(shape), dt))

    def psum(self, name, shape, dt):
        return self.es.enter_context(self.nc.psum_tensor(name, list(shape), dt))

    @staticmethod
    def _at(reads, writes):
        reads = tuple(reads)
        for k in tuple(writes) + reads:
            if isinstance(k, tuple) and k and k[0] == "ar":
                return reads + ("AT",)
        return reads

    def op(self, eng, fn, reads=(), writes=()):
        self.ops.append(dict(eng=eng, fn=fn, reads=self._at(reads, writes), writes=tuple(writes), dma=None))

    def dma(self, eng, fn, stream, reads=(), writes=()):
        self.ops.append(dict(eng=eng, fn=fn, reads=self._at(reads, writes), writes=tuple(writes), dma=stream))

    def emit(self, final_wait_streams=()):
        nc = self.nc
        ops = self.ops
        n = len(ops)
        last_w = {}
        readers = {}
        deps = [None] * n
        for i, o in enumerate(ops):
            d = set()
            for k in o["reads"]:
                if k in last_w:
                    d.add(last_w[k])
            for k in o["writes"]:
                if k in last_w:
                    d.add(last_w[k])
                for r in readers.get(k, {}).values():
                    d.add(r)
            d.discard(i)
            if o["eng"] == "pe" and o["dma"] is None:
                d = {j for j in d if not (ops[j]["eng"] == "pe" and ops[j]["dma"] is None)}
            best = {}
            for j in d:
                kk = ("dma", ops[j]["dma"]) if ops[j]["dma"] is not None else ops[j]["eng"]
                if best.get(kk, -1) < j:
                    best[kk] = j
            deps[i] = set(best.values())
            rk = ("dma", o["dma"]) if o["dma"] is not None else o["eng"]
            for k in o["reads"]:
                readers.setdefault(k, {})[rk] = i
            for k in o["writes"]:
                last_w[k] = i
                readers[k] = {}
        target = [False] * n
        for i in range(n):
            for j in deps[i]:
                target[j] = True
        eng_cnt = {e: 0 for e in ENGS}
        stream_cnt = {}
        ev = [None] * n
        for i, o in enumerate(ops):
            if o["dma"] is not None:
                s = o["dma"]
                stream_cnt[s] = stream_cnt.get(s, 0) + 1
                ev[i] = (("dma", s), 16 * stream_cnt[s])
            elif target[i]:
                eng_cnt[o["eng"]] += 1
                ev[i] = (("eng", o["eng"]), eng_cnt[o["eng"]])
        sems = {}
        for e in ENGS:
            sems[("eng", e)] = self.es.enter_context(nc.semaphore("sem_" + e))
        for s in stream_cnt:
            sems[("dma", s)] = self.es.enter_context(nc.semaphore("dsem_%s" % (s,)))
        per_eng = {e: [] for e in ENGS}
        seen = {e: {} for e in ENGS}
        issued = {}
        nwaits = 0
        for i, o in enumerate(ops):
            need = {}
            for j in deps[i]:
                sk, v = ev[j]
                if sk[0] == "dma":
                    v = 16 * issued[sk[1]]
                if need.get(sk, 0) < v:
                    need[sk] = v
            if o["dma"] is not None:
                issued[o["dma"]] = issued.get(o["dma"], 0) + 1
            waits = []
            for sk, v in need.items():
                if seen[o["eng"]].get(sk, 0) >= v:
                    continue
                seen[o["eng"]][sk] = v
                waits.append((sk, v))
            nwaits += len(waits)
            per_eng[o["eng"]].append((i, waits))
        self.stats = dict(n_ops=n, n_waits=nwaits, per_eng={e: len(per_eng[e]) for e in ENGS},
                          n_sems=len(sems), max_cnt=dict(eng_cnt))
        finals = [(("dma", s), 16 * stream_cnt[s]) for s in final_wait_streams]
        block = self.es.enter_context(nc.Block())

        def run(engname, engine):
            for i, waits in per_eng[engname]:
                o = ops[i]
                for sk, v in waits:
                    engine.wait_ge(sems[sk], v)
                ins = o["fn"](engine)
                if ev[i] is not None:
                    sk, v = ev[i]
                    ins.then_inc(sems[sk], 16 if o["dma"] is not None else 1)
            if engname == "sp":
                for sk, v in finals:
                    engine.wait_ge(sems[sk], v)

        @block.tensor
        def _(e):
            run("pe", e)

        @block.scalar
        def _(e):
            run("act", e)

        @block.vector
        def _(e):
            run("dve", e)

        @block.gpsimd
        def _(e):
            run("pool", e)

        @block.sync
        def _(e):
            run("sp", e)

    def close(self):
        self.es.close()


class WStream:
    def __init__(self, P, nslots, slot_elems, eng="pool"):
        self.P = P
        self.n = nslots
        self.slots = [P.sbuf("wslot%d" % i, [128, slot_elems], BF16) for i in range(nslots)]
        self.plan = []
        self.next_issue = 0
        self.next_use = 0
        self.eng = eng

    def add(self, fn):
        self.plan.append(fn)

    def _issue(self):
        if self.next_issue >= len(self.plan):
            return
        i = self.next_issue
        self.next_issue += 1
        s = i % self.n
        for d in self.plan[i](self.slots[s]):
            self.P.dma(self.eng, d, "W%d" % s, writes=[("W", s)])

    def start(self):
        for _ in range(self.n):
            self._issue()

    def acquire(self):
        i = self.next_use
        self.next_use += 1
        s = i % self.n
        return self.slots[s], ("W", s)

    def release(self):
        self._issue()


class Builder:
    def __init__(self, cfg):
        self.cfg = cfg
        nc = bass.Bass("TRN2", target_bir_lowering=False)
        self.nc = nc
        self.P = Prog(nc)
        self.dram = {}
        self.psc = 0
        self.dbg = []
        self.dbg_streams = []

    def din(self, name, shape, dt=F32):
        t = self.nc.dram_tensor(name, list(shape), dt, kind="ExternalInput").ap()
        self.dram[name] = t
        return t

    def setup(self):
        P = self.P
        nc = self.nc
        d = self.din
        d("x", [S, D])
        for nm in ("ffn1_norm", "mix_norm", "ffn2_norm"):
            d(nm, [DEPTH, D])
        d("ffn1_w_gu", [DEPTH, D, 2 * DFF])
        d("ffn1_w_down", [DEPTH, DFF, D])
        d("ffn2_w_gu", [DEPTH, D, 2 * DFF])
        d("ffn2_w_down", [DEPTH, DFF, D])
        d("final_norm", [D])
        d("c_ident", [128, 128])
        d("nsa_w_in", [2, D, 2608]); d("nsa_ck_pe", [2, 32, 64]); d("nsa_ck_w1", [2, 2048, 128]); d("nsa_ck_w2", [2, 128, 64])
        d("nsa_cv_pe", [2, 32, 64]); d("nsa_cv_w1", [2, 2048, 128]); d("nsa_cv_w2", [2, 128, 64]); d("nsa_w_out", [2, D, D])
        d("swa_w_in", [1, D, 1280]); d("swa_sinks", [1, 16]); d("swa_w_out", [1, D, D])
        d("fox_w_in", [1, D, 3088]); d("fox_b_f", [1, 16]); d("fox_w_out", [1, D, D])
        d("c_cos", [64, S]); d("c_sin", [64, S]); d("c_rot", [64, 64])
        d("c_cw", [128, 896], BF16); d("c_bw", [128, 128], BF16)
        d("c_mw", [128, S], BF16); d("c_ew", [32, S], BF16); d("c_fw", [128, 64]); d("c_kw", [128, 64]); d("c_ov", [128, 32], BF16)
        self.out = nc.dram_tensor("out", [S, D], F32, kind="ExternalOutput").ap()

        self.x_fm = P.sbuf("x_fm", [128, KC, S], F32)
        self.h = P.sbuf("h", [128, KC, S], BF16)
        self.ident = P.sbuf("ident", [128, 128], F32)
        self.identb = P.sbuf("identb", [128, 128], BF16)
        self.ones_bf = P.sbuf("ones_bf", [128, 128], BF16)
        self.onesf = P.sbuf("onesf", [128, 1], F32)
        self.fence_t = P.sbuf("fence_t", [128, 8], F32)
        self.gains = P.sbuf("gains", [128, 13, KC], F32)
        self.ps = [P.psum("ps%d" % i, [128, 512], F32) for i in range(8)]
        self.W = WStream(P, 4, 2048)
        self.ARENA = 48000
        self.arena = P.sbuf("arena", [128, self.ARENA], BF16)
        self.ab = 0
        self.a = self.aa([128, 11, S], BF16)
        self.sq = [self.aa([128, KC, 512], BF16)]
        self.rstd = [self.aa([128, 512], F32) for i in range(2)]
        self.sg = [self.aa([128, 512], F32) for i in range(2)]
        self.stage = [self.aa([128, D], F32) for i in range(2)]
        P.op("dve", lambda e: e.memset(self.onesf[:], 1.0), writes=["onesf"])

        P.dma("sp", lambda e: e.dma_start(out=self.ident[:], in_=self.dram["c_ident"]), self.ustream(), writes=["ident"])
        P.op("act", lambda e: e.copy(out=self.identb[:], in_=self.ident[:]), reads=["ident"], writes=["identb"])
        P.op("dve", lambda e: e.memset(self.ones_bf[:], 1.0), writes=["ones_bf"])
        self.graw = P.sbuf("graw", [104, 128], F32)
        for l in range(DEPTH):
            for j, nm in enumerate(("ffn1_norm", "mix_norm", "ffn2_norm")):
                idx = l * 3 + j
                src = self.dram[nm][l, :].rearrange("(c p) -> c p", p=128)
                P.dma("sp", (lambda idx, src: lambda e: e.dma_start(out=self.graw[idx * 8:(idx + 1) * 8, :], in_=src))(idx, src),
                      "graw", writes=["graw"])
        src = self.dram["final_norm"].rearrange("(c p) -> c p", p=128)
        P.dma("sp", lambda e: e.dma_start(out=self.graw[96:104, :], in_=src), "graw", writes=["graw"])
        P.op("pe", lambda e: e.matmul(self.ps[7][:, 0:104], lhsT=self.graw[:], rhs=self.ident[0:104, 0:104],
                                      start=True, stop=True), reads=["graw", "ident"], writes=[("ps", 7)])
        P.op("act", lambda e: e.copy(out=self.gains[:].rearrange("p a c -> p (a c)"), in_=self.ps[7][:, 0:104]),
             reads=[("ps", 7)], writes=["gains"])

    def aa(self, shape, dt):
        esz = 4 if dt == F32 else 2
        nel = 1
        for d_ in shape[1:]:
            nel *= d_
        off = (self.ab + 31) // 32 * 32
        self.ab = off + nel * esz
        assert self.ab <= self.ARENA * 2, ("arena overflow", self.ab)
        v = self.arena[0:shape[0], off // 2:(off + nel * esz) // 2]
        if dt == F32:
            v = v.bitcast(F32)
        if len(shape) == 3:
            v = v.rearrange("p (a b) -> p a b", a=shape[1])
        elif len(shape) == 4:
            v = v.rearrange("p (a b c) -> p a b c", a=shape[1], b=shape[2])
        return v

    def dump(self, name, view, keys, dt=F32):
        if not self.cfg.get("dbg"):
            return
        shp = [view.shape[0], int(np.prod(view.shape[1:]))]
        t = self.nc.dram_tensor("dbg_" + name, shp, dt, kind="ExternalOutput").ap()
        self.dbg.append("dbg_" + name)
        src = view
        if len(view.shape) == 3:
            t = t.rearrange("p (a b) -> p a b", a=view.shape[1])
        self.P.dma("sp", lambda e: e.dma_start(out=t, in_=src), "dbg_" + name, reads=list(keys))
        self.dbg_streams.append("dbg_" + name)

    def fence(self):
        self.P.op("dve", lambda e: e.memset(self.fence_t[:, 0:1], 0.0), writes=["AT"])

    def psb(self, lo=4, hi=8):
        b = lo + (self.psc % (hi - lo))
        self.psc += 1
        return b

    def plan_ffn(self, l, which):
        if self.cfg.get("noffn"):
            return
        wgu = self.dram["ffn%d_w_gu" % which][l].rearrange("(c p) n -> p c n", p=128)
        wdn = self.dram["ffn%d_w_down" % which][l]
        for gr in range(2):
            for jj in range(11):
                j = gr * 11 + jj

                def f(slot, j=j):
                    v = slot[:, 0:2048].rearrange("p (c n) -> p c n", c=KC)
                    return [lambda e: e.dma_start(out=v[:, :, 0:128], in_=wgu[:, :, j * 128:(j + 1) * 128]),
                            lambda e: e.dma_start(out=v[:, :, 128:256], in_=wgu[:, :, DFF + j * 128:DFF + (j + 1) * 128])]
                self.W.add(f)
            for m in range(KC):
                def f(slot, m=m, gr=gr):
                    v = slot[:, 0:11 * 128].rearrange("p (c n) -> p c n", c=11)
                    src = wdn[gr * 1408:(gr + 1) * 1408, m * 128:(m + 1) * 128].rearrange("(c p) n -> p c n", p=128)
                    return [lambda e: e.dma_start(out=v, in_=src)]
                self.W.add(f)

    def load_x(self):
        P = self.P
        x = self.dram["x"]
        for tt in range(NT):
            st = self.stage[tt % 2]
            sk = ("ar", "stage", tt % 2)
            P.dma("sp", (lambda tt, st: lambda e: e.dma_start(out=st[:], in_=x[tt * 128:(tt + 1) * 128, :]))(tt, st),
                  "xin%d" % (tt % 2), writes=[sk])
            for half in range(2):
                b = self.psb()
                for cc in range(4):
                    c = half * 4 + cc
                    P.op("pe", (lambda b, cc, c, st: lambda e: e.matmul(
                        self.ps[b][:, cc * 128:(cc + 1) * 128], lhsT=st[:, c * 128:(c + 1) * 128],
                        rhs=self.ident[:], start=True, stop=True))(b, cc, c, st),
                        reads=[sk, "ident"], writes=[("ps", b)])
                P.op("act", (lambda b, half, tt: lambda e: e.copy(
                    out=self.x_fm[:, half * 4:(half + 1) * 4, tt * 128:(tt + 1) * 128],
                    in_=self.ps[b][:].rearrange("p (c n) -> p c n", c=4)))(b, half, tt),
                    reads=[("ps", b)], writes=[("x", c4, tt // 4) for c4 in range(half * 4, half * 4 + 4)])

    def rmsnorm(self, gidx, out_fp32_inplace=False):
        P = self.P
        for tb in range(NB):
            blk = slice(tb * 512, (tb + 1) * 512)
            sq = self.sq[0]
            sqk = ("ar", "sq")
            rs = self.rstd[tb % 2]
            rk = ("ar", "rstd", tb % 2)
            xkeys = [("x", c, tb) for c in range(KC)]
            P.op("act", (lambda sq, blk: lambda e: e.activation(out=sq[:], in_=self.x_fm[:, :, blk], func=AF.Square))(sq, blk),
                 reads=xkeys, writes=[sqk])
            b = self.psb()
            for c in range(KC):
                P.op("pe", (lambda b, c, sq: lambda e: e.matmul(self.ps[b][:], lhsT=self.ones_bf[:], rhs=sq[:, c, :],
                                                                start=(c == 0), stop=(c == KC - 1)))(b, c, sq),
                     reads=[sqk, "ones_bf"], writes=[("ps", b)])
            P.op("act", (lambda b, rs: lambda e: e.activation(out=rs[:], in_=self.ps[b][:], func=AF.Ln,
                                                              scale=1.0 / D, bias=self.epsb[:]))(b, rs),
                 reads=[("ps", b), "epsb"], writes=[rk])
            P.op("act", (lambda rs: lambda e: e.activation(out=rs[:], in_=rs[:], func=AF.Exp, scale=-0.5))(rs), reads=[rk], writes=[rk])
            for c in range(KC):
                eng = "dve"
                if out_fp32_inplace:
                    dst = self.x_fm[:, c, blk]
                    wk = [("x", c, tb)]
                else:
                    dst = self.h[:, c, blk]
                    wk = [("h", c, tb)]
                P.op(eng, (lambda dst, c, blk, rs: lambda e: e.scalar_tensor_tensor(
                    out=dst, in0=self.x_fm[:, c, blk], scalar=self.gains[:, gidx, c:c + 1], in1=rs[:],
                    op0=ALU.mult, op1=ALU.mult))(dst, c, blk, rs),
                    reads=[("x", c, tb), rk, "gains"], writes=wk)

    def ffn(self, l, which):
        P = self.P
        if self.cfg.get("noffn"):
            return
        self.fence()
        self.rmsnorm(l * 3 + (0 if which == 1 else 2))
        gcount = 0
        for gr in range(2):
            for jj in range(11):
                wt, wk = self.W.acquire()
                wv = wt[:, 0:2048].rearrange("p (c n) -> p c n", c=KC)
                for tb in range(NB):
                    blk = slice(tb * 512, (tb + 1) * 512)
                    bg = (gcount % 2) * 2
                    bu = bg + 1
                    gcount += 1
                    for k in range(KC):
                        P.op("pe", (lambda bg, k, blk, wv: lambda e: e.matmul(
                            self.ps[bg][:], lhsT=wv[:, k, 0:128], rhs=self.h[:, k, blk],
                            start=(k == 0), stop=(k == KC - 1)))(bg, k, blk, wv),
                            reads=[wk, ("h", k, tb)], writes=[("ps", bg)])
                    for k in range(KC):
                        P.op("pe", (lambda bu, k, blk, wv: lambda e: e.matmul(
                            self.ps[bu][:], lhsT=wv[:, k, 128:256], rhs=self.h[:, k, blk],
                            start=(k == 0), stop=(k == KC - 1)))(bu, k, blk, wv),
                            reads=[wk, ("h", k, tb)], writes=[("ps", bu)])
                    sg = self.sg[gcount % 2]
                    sgk = ("ar", "sg", gcount % 2)
                    P.op("act", (lambda sg, bg: lambda e: e.activation(out=sg[:], in_=self.ps[bg][:], func=AF.Silu))(sg, bg),
                         reads=[("ps", bg)], writes=[sgk])
                    P.op("dve", (lambda sg, bu, jj, blk: lambda e: e.tensor_tensor(
                        out=self.a[:, jj, blk], in0=self.ps[bu][:], in1=sg[:], op=ALU.mult))(sg, bu, jj, blk),
                        reads=[("ps", bu), sgk], writes=[("ar", "a", jj, tb)])
                self.W.release()
            for m in range(KC):
                wt, wk = self.W.acquire()
                wv = wt[:, 0:11 * 128].rearrange("p (c n) -> p c n", c=11)
                for tb in range(NB):
                    blk = slice(tb * 512, (tb + 1) * 512)
                    b = self.psb()
                    for jj in range(11):
                        P.op("pe", (lambda b, jj, blk, wv: lambda e: e.matmul(
                            self.ps[b][:], lhsT=wv[:, jj, :], rhs=self.a[:, jj, blk],
                            start=(jj == 0), stop=(jj == 10)))(b, jj, blk, wv),
                            reads=[wk, ("ar", "a", jj, tb)], writes=[("ps", b)])
                    P.op("dve", (lambda b, m, blk: lambda e: e.scalar_tensor_tensor(
                        out=self.x_fm[:, m, blk], in0=self.ps[b][:], scalar=0.5, in1=self.x_fm[:, m, blk],
                        op0=ALU.mult, op1=ALU.add))(b, m, blk),
                        reads=[("ps", b), ("x", m, tb)], writes=[("x", m, tb)])
                self.W.release()

    def store_out(self):
        P = self.P
        self.fence()
        self.rmsnorm(12, out_fp32_inplace=True)
        for tt in range(NT):
            st = self.stage[tt % 2]
            sk = ("ar", "stage", tt % 2)
            for half in range(2):
                b = self.psb()
                for cc in range(4):
                    c = half * 4 + cc
                    P.op("pe", (lambda b, cc, c, tt: lambda e: e.matmul(
                        self.ps[b][:, cc * 128:(cc + 1) * 128], lhsT=self.x_fm[:, c, tt * 128:(tt + 1) * 128],
                        rhs=self.ident[:], start=True, stop=True))(b, cc, c, tt),
                        reads=[("x", c, tt // 4), "ident"], writes=[("ps", b)])
                P.op("act", (lambda b, half, st: lambda e: e.copy(
                    out=st[:, half * 512:(half + 1) * 512], in_=self.ps[b][:]))(b, half, st),
                    reads=[("ps", b)], writes=[sk])
            P.dma("sp", (lambda tt, st: lambda e: e.dma_start(out=self.out[tt * 128:(tt + 1) * 128, :], in_=st[:]))(tt, st),
                  "out%d" % (tt % 2), reads=[sk])

    def build(self):
        P = self.P
        self.setup()
        self.epsb = P.sbuf("epsb", [128, 1], F32)
        P.op("dve", lambda e: e.memset(self.epsb[:], EPS), writes=["epsb"])
        layers = self.cfg.get("layers", list(range(DEPTH)))
        phases = []
        for l in layers:
            phases += [("f", l, 1), ("m", l, 0), ("f", l, 2)]
        phases = phases[:self.cfg.get("nphase", len(phases))]
        for (k, l, w) in phases:
            if k == "f":
                self.plan_ffn(l, w)
            else:
                self.plan_mixer(l)
        self.W.start()
        self.load_x()
        for pi_, (k, l, w) in enumerate(phases):
            if k == "f":
                self.ffn(l, w)
            else:
                self.mixer(l)
            if self.cfg.get("dbg"):
                self.dump("px%d" % pi_, self.x_fm[:, 0, :], [("x", 0, t_) for t_ in range(NB)])
        self.store_out()
        P.emit(final_wait_streams=["out0", "out1"] + self.dbg_streams)
        P.close()
        return self.nc

    def plan_mixer(self, l):
        if self.cfg.get("nomix"):
            return
        kind = l % 3
        j = l // 3
        W = self.W
        if kind == 2:
            win = self.dram["fox_w_in"][j].rearrange("(c p) n -> p c n", p=128)
            wout = self.dram["fox_w_out"][j]

            def f(slot):
                v = slot[:, 0:KC * 16].rearrange("p (c n) -> p c n", c=KC)
                return [lambda e: e.dma_start(out=v, in_=win[:, :, 3072:3088])]
            W.add(f)
            for g in range(8):
                for part in range(3):
                    def f(slot, g=g, part=part):
                        v = slot[:, 0:KC * 128].rearrange("p (c n) -> p c n", c=KC)
                        c0 = part * 1024 + g * 128
                        return [lambda e: e.dma_start(out=v, in_=win[:, :, c0:c0 + 128])]
                    W.add(f)

                def f(slot, g=g):
                    v = slot[:, 0:1024]
                    return [lambda e: e.dma_start(out=v, in_=wout[g * 128:(g + 1) * 128, :])]
                W.add(f)
        elif kind == 1:
            win = self.dram["swa_w_in"][j].rearrange("(c p) n -> p c n", p=128)
            wout = self.dram["swa_w_out"][j]
            for g in range(4):
                kv = g // 2

                def f(slot, g=g):
                    v = slot[:, 0:KC * 256].rearrange("p (c n) -> p c n", c=KC)
                    return [lambda e: e.dma_start(out=v, in_=win[:, :, g * 256:(g + 1) * 256])]
                W.add(f)

                def f(slot, kv=kv):
                    v = slot[:, 0:KC * 128].rearrange("p (c n) -> p c n", c=KC)
                    return [lambda e: e.dma_start(out=v[:, :, 0:64], in_=win[:, :, 1024 + kv * 64:1024 + (kv + 1) * 64]),
                            lambda e: e.dma_start(out=v[:, :, 64:128], in_=win[:, :, 1152 + kv * 64:1152 + (kv + 1) * 64])]
                W.add(f)

                def f(slot, g=g):
                    v = slot[:, 0:2048].rearrange("p (a n) -> p a n", a=2)
                    src = wout[g * 256:(g + 1) * 256, :].rearrange("(a p) n -> p a n", p=128)
                    return [lambda e: e.dma_start(out=v, in_=src)]
                W.add(f)
        else:
            self.plan_nsa(l)

    def mixer(self, l):
        if self.cfg.get("nomix"):
            return
        kind = l % 3
        self.fence()
        if self.cfg.get("dbg") and l == 1:
            self.dump("r_x0", self.x_fm[:, 0, :], [("x", 0, t_) for t_ in range(NB)])
        self.rmsnorm(l * 3 + 1)
        if self.cfg.get("dbg") and l == 1:
            self.dump("r_h0", self.h[:, 0, :], [("h", 0, t_) for t_ in range(NB)], BF16)
            self.dump("r_rstd0", self.rstd[0][:], [("ar", "rstd", 0)])
            self.dump("r_rstd1", self.rstd[1][:], [("ar", "rstd", 1)])
            self.dump("r_sq", self.sq[0][:], [("ar", "sq")], BF16)
            self.dump("r_gains", self.gains[:], ["gains"])
            self.dump("r_epsb", self.epsb[:], ["epsb"])
        self.fence()
        self.ab = 0
        self.scc = 0
        if self.cfg.get("mixstub") and l == 1:
            n = {0: 37, 1: 12, 2: 33}[kind]
            for _ in range(n):
                wt, wk = self.W.acquire()
                self.P.op("dve", (lambda wt: lambda e: e.tensor_copy(out=self.fence_t[:, 1:2], in_=wt[:, 0:1]))(wt), reads=[wk], writes=["fdummy"])
                self.W.release()
            return
        if kind == 2:
            self.fox(l)
        elif kind == 1:
            self.swa(l)
        else:
            self.nsa(l)

    def ustream(self):
        self.usc = getattr(self, "usc", 0) + 1
        return "u%d" % self.usc

    def cload(self, name, view, key):
        self.P.dma("sp", lambda e: e.dma_start(out=view, in_=self.dram[name]), self.ustream(), writes=[key])

    def proj_fm(self, wv, wk, c0, ncols, post):
        P = self.P
        for tb in range(NB):
            blk = slice(tb * 512, (tb + 1) * 512)
            b = self.psb(6, 8)
            for k in range(KC):
                P.op("pe", (lambda b, k, blk: lambda e: e.matmul(
                    self.ps[b][0:ncols, :], lhsT=wv[:, k, c0:c0 + ncols], rhs=self.h[:, k, blk],
                    start=(k == 0), stop=(k == KC - 1)))(b, k, blk),
                    reads=[wk, ("h", k, tb)], writes=[("ps", b)])
            post(tb, b)

    def proj_tm(self, wv, wk, c0, vx, vkey):
        P = self.P
        for half in range(2):
            b = self.psb(6, 8)
            for t8 in range(8):
                tt = half * 8 + t8
                for k in range(KC):
                    P.op("pe", (lambda b, k, tt, t8: lambda e: e.matmul(
                        self.ps[b][:, t8 * 64:(t8 + 1) * 64], lhsT=self.h[:, k, tt * 128:(tt + 1) * 128],
                        rhs=wv[:, k, c0:c0 + 64], start=(k == 0), stop=(k == KC - 1)))(b, k, tt, t8),
                        reads=[wk, ("h", k, tt // 4)], writes=[("ps", b)])
            P.op("act", (lambda b, half: lambda e: e.copy(
                out=vx[:, half * 8:(half + 1) * 8, 0:64],
                in_=self.ps[b][:].rearrange("p (t n) -> p t n", t=8)))(b, half),
                reads=[("ps", b)], writes=[vkey])

    def rope_post(self, dst_fn, dkey, scale):
        P = self.P
        cosT, sinT, rotm = self.cosT, self.sinT, self.rotm

        def post(tb, b):
            blk = slice(tb * 512, (tb + 1) * 512)
            i = self.scc % len(self.rsc)
            self.scc += 1
            qf, t1, t2 = self.rsc[i]
            kq, k1, k2 = ("ar", "qf", i), ("ar", "t1", i), ("ar", "t2", i)
            P.op("act", lambda e: e.mul(out=qf[:], in_=self.ps[b][0:64, :], mul=scale), reads=[("ps", b)], writes=[kq])
            b2 = self.psb(6, 8)
            P.op("pe", lambda e: e.matmul(self.ps[b2][0:64, :], lhsT=rotm[:], rhs=qf[:], start=True, stop=True),
                 reads=[kq, ("ar", "rotm")], writes=[("ps", b2)])
            P.op("dve", lambda e: e.tensor_tensor(out=t1[:], in0=qf[:], in1=cosT[:, blk], op=ALU.mult),
                 reads=[kq, ("ar", "rope")], writes=[k1])
            P.op("dve", lambda e: e.tensor_tensor(out=t2[:], in0=self.ps[b2][0:64, :], in1=sinT[:, blk], op=ALU.mult),
                 reads=[("ps", b2), ("ar", "rope")], writes=[k2])
            dst = dst_fn(tb)
            t1v = t1[:].rearrange("p (a b) -> p a b", a=4)
            t2v = t2[:].rearrange("p (a b) -> p a b", a=4)
            P.op("dve", lambda e: e.tensor_tensor(out=dst, in0=t1v, in1=t2v, op=ALU.add),
                 reads=[k1, k2], writes=[dkey])
        return post

    def setup_rope(self):
        self.cosT = self.aa([64, S], F32)
        self.sinT = self.aa([64, S], F32)
        self.rotm = self.aa([64, 64], F32)
        self.cload("c_cos", self.cosT[:], ("ar", "rope"))
        self.cload("c_sin", self.sinT[:], ("ar", "rope"))
        self.cload("c_rot", self.rotm[:], ("ar", "rotm"))
        self.rsc = [tuple(self.aa([64, 512], F32) for _ in range(3)) for _ in range(2)]

    def attn(self, qrhs, qkeys, steps, den_extra=None, n=512, bo=None, sbanks=(0, 1, 2, 3), hooks=None):
        P = self.P
        if bo is None:
            bo = 4 + (self.oc % 2)
            self.oc += 1
        ns = len(steps)
        sbank = [None] * ns
        step_bo = [(st[5] if len(st) > 5 else bo) for st in steps]
        first_i, last_i = {}, {}
        for i_, b_ in enumerate(step_bo):
            first_i.setdefault(b_, i_)
            last_i[b_] = i_
        steps = [st[:5] for st in steps]
        hooks = hooks or {}

        def scores(si):
            kT, kkeys, masks, vap, vkeys = steps[si]
            bs = sbanks[self.sc % len(sbanks)]
            self.sc += 1
            sbank[si] = bs
            rsel = getattr(self, "_rhs_sel", None)
            q_ap, q_keys = (rsel[si] if rsel else (qrhs, qkeys))
            P.op("pe", lambda e: e.matmul(self.ps[bs][:, 0:n], lhsT=kT, rhs=q_ap, start=True, stop=(len(masks) == 0)),
                 reads=list(kkeys) + list(q_keys), writes=[("ps", bs)])
            for mi, (ml, mr, mk) in enumerate(masks):
                P.op("pe", (lambda ml, mr, last: lambda e: e.matmul(self.ps[bs][:, 0:n], lhsT=ml, rhs=mr, start=False,
                                                                    stop=last))(ml, mr, mi == len(masks) - 1),
                     reads=list(mk), writes=[("ps", bs)])

        def expv(si):
            kT, kkeys, masks, vap, vkeys = steps[si]
            bs = sbank[si]
            pi = self.ptc % 3
            self.ptc += 1
            pt = self.pt[pi]
            pk = ("ar", "pt", pi)
            P.op("act", lambda e: e.activation(out=pt[:, 0:n], in_=self.ps[bs][:, 0:n], func=AF.Exp),
                 reads=[("ps", bs)], writes=[pk])
            bo_s = step_bo[si]
            last = (si == last_i[bo_s]) and den_extra is None
            P.op("pe", lambda e: e.matmul(self.ps[bo_s][:, 0:n], lhsT=vap, rhs=pt[:, 0:n], start=(si == first_i[bo_s]), stop=last),
                 reads=list(vkeys) + [pk], writes=[("ps", bo_s)])
            for _ in range(self.cfg.get("dummy", 0)):
                P.op("pe", lambda e: e.matmul(self.ps[bs][:, 0:n], lhsT=vap, rhs=pt[:, 0:n], start=True, stop=True),
                     reads=list(vkeys) + [pk], writes=[("ps", bs)])

        la = min(2, len(sbanks) - 1)
        for si in range(min(la, ns)):
            scores(si)
        for si in range(ns):
            if si + la < ns:
                scores(si + la)
            expv(si)
            if si in hooks:
                for hk in hooks[si]:
                    hk()
        if den_extra is not None:
            dl, dr, dk = den_extra
            P.op("pe", lambda e: e.matmul(self.ps[bo][:, 0:n], lhsT=dl, rhs=dr, start=False, stop=True),
                 reads=list(dk), writes=[("ps", bo)])
        return bo

    def attn_mixed(self, q64, q96, qt, steps, hooks, sbanks):
        par = qt % 2
        self._rhs_sel = [(q64, [("ar", "qt")]) if st[5] == 4 else (q96, [("ar", "qt"), ("ar", "qsel", par)]) for st in steps]
        try:
            return self.attn(None, None, steps, bo=None, sbanks=sbanks, hooks=hooks)
        finally:
            self._rhs_sel = None

    def out_proj(self, wv, wk, npairs, ot, okey):
        P = self.P
        for m in range(KC):
            for tb in range(NB):
                blk = slice(tb * 512, (tb + 1) * 512)
                b = self.psb(6, 8)
                for p in range(npairs):
                    P.op("pe", (lambda b, p, m, blk: lambda e: e.matmul(
                        self.ps[b][:], lhsT=wv[:, p, m * 128:(m + 1) * 128], rhs=ot[:, p, blk],
                        start=(p == 0), stop=(p == npairs - 1)))(b, p, m, blk),
                        reads=[wk, okey], writes=[("ps", b)])
                P.op("dve", (lambda b, m, blk: lambda e: e.tensor_tensor(
                    out=self.x_fm[:, m, blk], in0=self.ps[b][:], in1=self.x_fm[:, m, blk], op=ALU.add))(b, m, blk),
                    reads=[("ps", b), ("x", m, tb)], writes=[("x", m, tb)])

    def mix_common(self):
        self.sc = 0
        self.oc = 0
        self.ptc = 0
        self.pt = [self.aa([128, 512], BF16) for _ in range(3)]
        self.rd = [self.aa([64, 512], F32) for _ in range(2)]
        self.o32 = [self.aa([64, 512], F32) for _ in range(2)]
        self.fc = 0

    def finalize_simple(self, bo, n, dsts):
        P = self.P
        i = self.fc % 2
        self.fc += 1
        rd, o32 = self.rd[i], self.o32[i]
        kr, ko = ("ar", "rd", i), ("ar", "o32", i)
        P.op("act", lambda e: e.activation(out=rd[:, 0:n], in_=self.ps[bo][64:128, 0:n], func=AF.Ln), reads=[("ps", bo)], writes=[kr])
        P.op("act", lambda e: e.activation(out=rd[:, 0:n], in_=rd[:, 0:n], func=AF.Exp, scale=-1.0), reads=[kr], writes=[kr])
        P.op("dve", lambda e: e.tensor_tensor(out=o32[:, 0:n], in0=self.ps[bo][0:64, 0:n], in1=rd[:, 0:n], op=ALU.mult),
             reads=[("ps", bo), kr], writes=[ko])
        for (dst, c0, nc_, dk) in dsts:
            src = o32[:, c0:c0 + nc_]
            if len(dst.shape) == 3:
                src = src.rearrange("p (a b) -> p a b", a=dst.shape[1])
            P.op("act", (lambda dst, src: lambda e: e.copy(out=dst, in_=src))(dst, src), reads=[ko], writes=[dk])

    def fox(self, l):
        P = self.P
        j = l // 3
        self.mix_common()
        cw = self.aa([128, 896], BF16)
        self.cload("c_cw", cw[:], ("ar", "cw"))
        QT = [self.aa([128, S], BF16) for _ in range(2)]
        KT = [self.aa([128, S], BF16) for _ in range(2)]
        VX = [self.aa([128, NT, 128], BF16) for _ in range(2)]
        OT = self.aa([128, 1, S], BF16)
        spz = self.aa([16, S], F32)
        cn = self.aa([16, S], F32)
        zer = self.aa([16, 512], F32)
        csp = self.aa([16, 6, S], BF16)
        negb = self.aa([16, 1], F32)
        for hh in range(2):
            P.op("dve", (lambda hh: lambda e: e.memset(QT[hh][64:128, :], 0.0))(hh), writes=[("ar", "qaux", hh)])
            P.op("dve", (lambda hh: lambda e: e.memset(KT[hh][64:128, :], 0.0))(hh), writes=[("ar", "kaux", hh)])
            P.op("dve", (lambda hh: lambda e: e.memset(QT[hh][64:70, :], 1.0))(hh), writes=[("ar", "qaux", hh)])
            P.op("dve", (lambda hh: lambda e: e.memset(KT[hh][64:70, :], 1.0))(hh), writes=[("ar", "kaux", hh)])
            P.op("pool", (lambda hh: lambda e: e.memset(VX[hh][:, :, 64:128], 1.0))(hh), writes=[("ar", "vx1", hh)])
        P.op("dve", lambda e: e.memset(zer[:], 0.0), writes=[("ar", "zer")])
        bsrc = self.dram["fox_b_f"][j, :].rearrange("(p o) -> p o", o=1)
        P.dma("sp", lambda e: e.dma_start(out=negb[:], in_=bsrc), self.ustream(), writes=[("ar", "negb")])
        P.op("act", lambda e: e.mul(out=negb[:], in_=negb[:], mul=-1.0), reads=[("ar", "negb")], writes=[("ar", "negb")])
        wt, wk = self.W.acquire()
        wv = wt[:, 0:KC * 16].rearrange("p (c n) -> p c n", c=KC)

        def post_gate(tb, b):
            blk = slice(tb * 512, (tb + 1) * 512)
            P.op("act", lambda e: e.activation(out=spz[:, blk], in_=self.ps[b][0:16, :], func=AF.Exp, scale=-1.0, bias=negb[:]),
                 reads=[("ps", b), ("ar", "negb")], writes=[("ar", "spz", tb)])
            P.op("act", lambda e: e.activation(out=spz[:, blk], in_=spz[:, blk], func=AF.Ln, bias=self.onesf[0:16, :]),
                 reads=[("ar", "spz", tb), "onesf"], writes=[("ar", "spz", tb)])
            init = 0.0 if tb == 0 else cn[:, tb * 512 - 1:tb * 512]
            P.op("dve", lambda e: e.tensor_tensor_scan(out=cn[:, blk], data0=zer[:], data1=spz[:, blk], initial=init,
                                                       op0=ALU.add, op1=ALU.add),
                 reads=[("ar", "spz", tb), ("ar", "zer"), ("ar", "cn")], writes=[("ar", "cn")])
        self.proj_fm(wv, wk, 0, 16, post_gate)
        self.W.release()
        r = spz
        P.op("dve", lambda e: e.tensor_copy(out=csp[:, 0, :], in_=cn[:]), reads=[("ar", "cn")], writes=[("ar", "csp")])
        P.op("dve", lambda e: e.tensor_tensor(out=r[:], in0=cn[:], in1=csp[:, 0, :], op=ALU.subtract),
             reads=[("ar", "cn"), ("ar", "csp")] + [("ar", "spz", t) for t in range(4)], writes=[("ar", "r")])
        P.op("dve", lambda e: e.tensor_copy(out=csp[:, 1, :], in_=r[:]), reads=[("ar", "r")], writes=[("ar", "csp")])
        P.op("dve", lambda e: e.tensor_tensor(out=r[:], in0=r[:], in1=csp[:, 1, :], op=ALU.subtract),
             reads=[("ar", "r"), ("ar", "csp")], writes=[("ar", "r")])
        P.op("dve", lambda e: e.tensor_copy(out=csp[:, 2, :], in_=r[:]), reads=[("ar", "r")], writes=[("ar", "csp")])
        P.op("dve", lambda e: e.tensor_scalar(out=csp[:, 3:6, :], in0=csp[:, 0:3, :], scalar1=-1.0, scalar2=None, op0=ALU.mult),
             reads=[("ar", "csp")], writes=[("ar", "csp")])
        for g in range(8):
            for part in range(3):
                wt, wk = self.W.acquire()
                wv = wt[:, 0:KC * 128].rearrange("p (c n) -> p c n", c=KC)
                for hh in range(2):
                    if part == 0:
                        def post(tb, b, hh=hh):
                            blk = slice(tb * 512, (tb + 1) * 512)
                            P.op("act", lambda e: e.mul(out=QT[hh][0:64, blk], in_=self.ps[b][0:64, :], mul=0.125),
                                 reads=[("ps", b)], writes=[("ar", "qt", hh)])
                        self.proj_fm(wv, wk, hh * 64, 64, post)
                    elif part == 1:
                        def post(tb, b, hh=hh):
                            blk = slice(tb * 512, (tb + 1) * 512)
                            P.op("act", lambda e: e.copy(out=KT[hh][0:64, blk], in_=self.ps[b][0:64, :]),
                                 reads=[("ps", b)], writes=[("ar", "kt", hh)])
                        self.proj_fm(wv, wk, hh * 64, 64, post)
                    else:
                        self.proj_tm(wv, wk, hh * 64, VX[hh], ("ar", "vx", hh))
                self.W.release()
            for hh in range(2):
                hd = g * 2 + hh
                for i3 in range(3):
                    P.dma("sp", (lambda hh, hd, i3: lambda e: e.dma_start(out=QT[hh][64 + i3:65 + i3, :], in_=csp[hd:hd + 1, 3 + i3, :]))(hh, hd, i3),
                          "aux", reads=[("ar", "csp")], writes=[("ar", "qaux", hh)])
                    P.dma("sp", (lambda hh, hd, i3: lambda e: e.dma_start(out=KT[hh][67 + i3:68 + i3, :], in_=csp[hd:hd + 1, i3, :]))(hh, hd, i3),
                          "aux", reads=[("ar", "csp")], writes=[("ar", "kaux", hh)])
            for hh in range(2):
                for qb in range(NB):
                    blk = slice(qb * 512, (qb + 1) * 512)
                    steps = []
                    for kt in range(4 * qb + 4):
                        masks = []
                        m = kt - 4 * qb
                        if m >= 0:
                            masks.append((self.identb[:], cw[:, 384 - m * 128:896 - m * 128], ["identb", ("ar", "cw")]))
                        steps.append((KT[hh][0:70, kt * 128:(kt + 1) * 128], [("ar", "kt", hh), ("ar", "kaux", hh)], masks,
                                      VX[hh][:, kt, :], [("ar", "vx", hh), ("ar", "vx1", hh)]))
                    bo = self.attn(QT[hh][0:70, blk], [("ar", "qt", hh), ("ar", "qaux", hh)], steps)
                    self.finalize_simple(bo, 512, [(OT[hh * 64:(hh + 1) * 64, 0, blk], 0, 512, ("ar", "ot"))])
            wt, wk = self.W.acquire()
            wv = wt[:, 0:1024].rearrange("p (a n) -> p a n", a=1)
            self.out_proj(wv, wk, 1, OT, ("ar", "ot"))
            self.W.release()
        self.dump("cn", cn[:], [("ar", "cn")])
        self.dump("spz", spz[:], [("ar", "r")])
        self.dump("qt", QT[0][:], [("ar", "qt", 0), ("ar", "qaux", 0)], BF16)
        self.dump("kt", KT[0][:], [("ar", "kt", 0), ("ar", "kaux", 0)], BF16)
        self.dump("vx", VX[0][:], [("ar", "vx", 0), ("ar", "vx1", 0)], BF16)
        self.dump("ot", OT[:], [("ar", "ot")], BF16)
        self.dump("pt", self.pt[0][:], [("ar", "pt", 0)], BF16)
        self.dump("rd", self.rd[0][:], [("ar", "rd", 0)])
        self.dump("o32", self.o32[0][:], [("ar", "o32", 0)])

    def swa(self, l):
        P = self.P
        self.mix_common()
        self.setup_rope()
        cw = self.aa([128, 896], BF16)
        self.cload("c_cw", cw[:], ("ar", "cw"))
        bw = self.aa([128, 128], BF16)
        self.cload("c_bw", bw[:], ("ar", "bw"))
        QT = self.aa([64, NT, 4, 128], BF16)
        KT = self.aa([64, S], BF16)
        VX = self.aa([128, NT, 128], BF16)
        OT = self.aa([128, 2, S], BF16)
        esk32 = self.aa([32, 16], F32)
        e0132 = self.aa([32, 128], F32)
        esk = esk32[0:1, :]
        e01 = e0132[0:1, :]
        P.op("pool", lambda e: e.memset(VX[:, :, 64:128], 1.0), writes=[("ar", "vx1")])
        P.op("dve", lambda e: e.memset(e0132[:], 0.0), writes=[("ar", "e01")])
        P.op("dve", lambda e: e.memset(e0132[0:1, 64:128], 1.0), writes=[("ar", "e01")])
        P.op("dve", lambda e: e.memset(esk32[:], 0.0), writes=[("ar", "esk")])
        P.dma("sp", lambda e: e.dma_start(out=esk32[0:1, :], in_=self.dram["swa_sinks"][0:1, :]), self.ustream(), writes=[("ar", "esk")])
        P.op("act", lambda e: e.activation(out=esk32[0:1, :], in_=esk32[0:1, :], func=AF.Exp), reads=[("ar", "esk")], writes=[("ar", "esk")])
        order = [0, 2, 1, 3]
        for g in range(4):
            wt, wk = self.W.acquire()
            wv = wt[:, 0:KC * 256].rearrange("p (c n) -> p c n", c=KC)
            for pos, r in enumerate(order):
                def dst_fn(tb, pos=pos):
                    return QT[:, tb * 4:(tb + 1) * 4, pos, :]
                self.proj_fm(wv, wk, r * 64, 64, self.rope_post(dst_fn, ("ar", "qt"), 0.125))
            self.W.release()
            wt, wk = self.W.acquire()
            wv = wt[:, 0:KC * 128].rearrange("p (c n) -> p c n", c=KC)
            self.proj_fm(wv, wk, 0, 64, self.rope_post(lambda tb: KT[:, tb * 512:(tb + 1) * 512].rearrange("p (a b) -> p a b", a=4),
                                                       ("ar", "kt"), 1.0))
            self.proj_tm(wv, wk, 64, VX, ("ar", "vx"))
            self.W.release()
            for qt in range(NT):
                steps = []
                for kt in (qt - 1, qt):
                    if kt < 0:
                        continue
                    mt = cw[:, 384:512] if kt == qt else bw[:]
                    mr = mt.rearrange("p (o n) -> p o n", o=1).to_broadcast([128, 4, 128])
                    steps.append((KT[:, kt * 128:(kt + 1) * 128], [("ar", "kt")],
                                  [(self.identb[:], mr, ["identb", ("ar", "cw"), ("ar", "bw")])],
                                  VX[:, kt, :], [("ar", "vx"), ("ar", "vx1")]))
                hd0 = g * 4
                sk = self.aa_sinkrow(esk, hd0, order)
                bo = self.attn(QT[:, qt, :, :], [("ar", "qt")], steps, den_extra=(e01, sk, [("ar", "e01"), ("ar", "esk")]))
                tsl = slice(qt * 128, (qt + 1) * 128)
                self.finalize_simple(bo, 512, [(OT[0:64, :, tsl], 0, 256, ("ar", "ot")),
                                               (OT[64:128, :, tsl], 256, 256, ("ar", "ot"))])
            wt, wk = self.W.acquire()
            wv = wt[:, 0:2048].rearrange("p (a n) -> p a n", a=2)
            self.out_proj(wv, wk, 2, OT, ("ar", "ot"))
            self.W.release()
        self.dump("s_qt", QT[:].rearrange("p a b c -> p (a b c)"), [("ar", "qt")], BF16)
        self.dump("s_kt", KT[:], [("ar", "kt")], BF16)
        self.dump("s_vx", VX[:], [("ar", "vx"), ("ar", "vx1")], BF16)
        self.dump("s_ot", OT[:], [("ar", "ot")], BF16)
        self.dump("s_esk", esk32[:], [("ar", "esk")])
        self.dump("s_e01", e0132[:], [("ar", "e01")])
        self.dump("s_rd", self.rd[0][:], [("ar", "rd", 0)])
        self.dump("s_o32", self.o32[0][:], [("ar", "o32", 0)])
        self.dump("s_pt", self.pt[0][:], [("ar", "pt", 0)], BF16)
        self.dump("s_h", self.h[:], [("h", c_, t_) for c_ in range(KC) for t_ in range(NB)], BF16)
        self.dump("s_cos", self.cosT[:], [("ar", "rope")])
        self.dump("s_cw", cw[:], [("ar", "cw")], BF16)
        self.dump("s_bw", bw[:], [("ar", "bw")], BF16)

    def aa_sinkrow(self, esk, hd0, order):
        v = esk[:, hd0:hd0 + 4].rearrange("p (a b) -> p b a", b=2)
        return v.rearrange("p b (a o) -> p b a o", o=1).to_broadcast([1, 2, 2, 128])

    def plan_nsa(self, l):
        j = l // 3
        W = self.W
        win = self.dram["nsa_w_in"][j].rearrange("(c p) n -> p c n", p=128)
        wout = self.dram["nsa_w_out"][j]

        def f(slot):
            v = slot[:, 0:KC * 48].rearrange("p (c n) -> p c n", c=KC)
            return [lambda e: e.dma_start(out=v, in_=win[:, :, 2560:2608])]
        W.add(f)
        for g in range(4):
            def kvcol(b, kvi, g=g):
                return 1024 + ((b * 2 + kvi) * 4 + g) * 64

            def f(slot, g=g):
                v = slot[:, 0:KC * 256].rearrange("p (c n) -> p c n", c=KC)
                return [lambda e: e.dma_start(out=v, in_=win[:, :, g * 256:(g + 1) * 256])]
            W.add(f)

            def f(slot, kvcol=kvcol):
                v = slot[:, 0:KC * 256].rearrange("p (c n) -> p c n", c=KC)
                cols = [kvcol(0, 0), kvcol(1, 0), kvcol(2, 0), kvcol(0, 1)]
                return [(lambda i, c0: lambda e: e.dma_start(out=v[:, :, i * 64:(i + 1) * 64], in_=win[:, :, c0:c0 + 64]))(i, c0)
                        for i, c0 in enumerate(cols)]
            W.add(f)

            def f(slot, kvcol=kvcol):
                v = slot[:, 0:KC * 128].rearrange("p (c n) -> p c n", c=KC)
                cols = [kvcol(1, 1), kvcol(2, 1)]
                return [(lambda i, c0: lambda e: e.dma_start(out=v[:, :, i * 64:(i + 1) * 64], in_=win[:, :, c0:c0 + 64]))(i, c0)
                        for i, c0 in enumerate(cols)]
            W.add(f)
            def addw1(nm):
                w1 = self.dram[nm][j].rearrange("(l d) n -> d l n", d=64)
                for half in range(2):
                    def f(slot, w1=w1, half=half):
                        v = slot[0:64, 0:2048].rearrange("p (l n) -> p l n", l=16)
                        return [lambda e: e.dma_start(out=v, in_=w1[:, half * 16:(half + 1) * 16, :])]
                    W.add(f)
            addw1("nsa_ck_w1")

            def f(slot):
                return [lambda e: e.dma_start(out=slot[:, 0:64], in_=self.dram["nsa_ck_w2"][j]),
                        lambda e: e.dma_start(out=slot[:, 64:128], in_=self.dram["nsa_cv_w2"][j]),
                        lambda e: e.dma_start(out=slot[0:64, 128:160], in_=self.dram["nsa_ck_pe"][j].rearrange("l d -> d l"),
                                              allow_slow_non_contiguous=True),
                        lambda e: e.dma_start(out=slot[0:64, 160:192], in_=self.dram["nsa_cv_pe"][j].rearrange("l d -> d l"),
                                              allow_slow_non_contiguous=True)]
            W.add(f)
            addw1("nsa_cv_w1")

            def f(slot, g=g):
                v = slot[:, 0:2048].rearrange("p (a n) -> p a n", a=2)
                src = wout[g * 256:(g + 1) * 256, :].rearrange("(a p) n -> p a n", p=128)
                return [lambda e: e.dma_start(out=v, in_=src)]
            W.add(f)

    def nsa(self, l):
        P = self.P
        self.sc = 0
        self.oc = 0
        self.ptc = 0
        self.fc = 0
        self.pt = [self.aa([128, 512], BF16) for _ in range(3)]
        self.cosT = self.aa([64, S], F32)
        self.sinT = self.aa([64, S], F32)
        self.rotm = self.aa([64, 64], F32)
        self.cload("c_cos", self.cosT[:], ("ar", "rope"))
        self.cload("c_sin", self.sinT[:], ("ar", "rope"))
        self.cload("c_rot", self.rotm[:], ("ar", "rotm"))
        self.rsc = [tuple(self.aa([64, 512], F32) for _ in range(3))]
        cw = self.aa([128, 896], BF16); self.cload("c_cw", cw[:], ("ar", "cw"))
        bw = self.aa([128, 128], BF16); self.cload("c_bw", bw[:], ("ar", "bw"))
        mw = self.aa([128, S], BF16); self.cload("c_mw", mw[:], ("ar", "mw"))
        fw = self.aa([128, 64], F32); self.cload("c_fw", fw[:], ("ar", "fw"))
        kw = self.aa([128, 64], F32); self.cload("c_kw", kw[:], ("ar", "fw"))
        ov = self.aa([128, 32], BF16); self.cload("c_ov", ov[:], ("ar", "ov"))
        QTX = self.aa([96, NT, 4, 128], BF16)
        QT = QTX[0:64]
        KT1X = self.aa([96, S], BF16)
        self.cload("c_ew", KT1X[64:96, :], ("ar", "ew"))
        KT = [None, KT1X[0:64, :], self.aa([64, S], BF16)]
        VX = [None, self.aa([128, NT, 128], BF16), self.aa([128, NT, 128], BF16)]
        OT = self.aa([128, 2, S], BF16)
        KT0 = OT[0:64, 0, :]
        V0T = OT[0:64, 1, :]
        GT64 = self.aa([64, S], BF16)
        GT = GT64[0:48, :]
        KTc = self.aa([64, 128], BF16)
        VXc = self.aa([128, 128], BF16)
        gel = [self.aa([128, 128], F32) for _ in range(3)]
        gbf = self.aa([128, 128], BF16)
        rd = self.aa([64, 512], F32)
        rg = self.aa([64, 512], F32)
        acc = self.aa([64, 512], F32)
        acc2 = self.aa([64, 512], F32)
        rd2 = self.aa([64, 512], F32)
        self.rsc = [self.rsc[0], (rd, rg, acc)]
        impn = self.aa([32, 512], F32)
        impT = self.aa([32, 128], F32)
        im1 = self.aa([128, 32], F32)
        im2 = self.aa([128, 32], F32)
        m8 = self.aa([128, 16], F32)
        seln = self.aa([128, 32], F32)
        e0132 = self.aa([32, 128], BF16)
        e01 = e0132[0:1, :]
        tiny32 = self.aa([32, 8], BF16)
        tiny_ = tiny32[0:1, :]
        tiny = tiny_[:, 0:1].to_broadcast([1, 512])
        OTK = ("ar", "ot")
        for b_ in (1, 2):
            P.op("pool", (lambda b_: lambda e: e.memset(VX[b_][:, :, 64:128], 1.0))(b_), writes=[("ar", "vx1", b_)])
        P.op("dve", lambda e: e.memset(e0132[:], 0.0), writes=[("ar", "e01")])
        P.op("dve", lambda e: e.memset(e0132[0:1, 64:128], 1.0), writes=[("ar", "e01")])
        P.op("dve", lambda e: e.memset(tiny32[:], 0.0), writes=[("ar", "e01")])
        P.op("dve", lambda e: e.memset(tiny32[0:1, :], 1e-30), writes=[("ar", "e01")])
        P.op("dve", lambda e: e.memset(GT64[:], 0.0), writes=[("ar", "gt")])
        P.op("dve", lambda e: e.memset(KTc[:], 0.0), writes=[("ar", "ktc")])
        P.op("dve", lambda e: e.memset(VXc[:], 0.0), writes=[("ar", "vxc")])
        P.op("dve", lambda e: e.memset(VXc[:, 64:128], 1.0), writes=[("ar", "vxc")])
        wt, wk = self.W.acquire()
        wv = wt[:, 0:KC * 48].rearrange("p (c n) -> p c n", c=KC)

        def post_gate(tb, b):
            blk = slice(tb * 512, (tb + 1) * 512)
            P.op("act", lambda e: e.activation(out=GT[:, blk], in_=self.ps[b][0:48, :], func=AF.Sigmoid),
                 reads=[("ps", b)], writes=[("ar", "gt")])
        self.proj_fm(wv, wk, 0, 48, post_gate)
        self.W.release()
        order = [0, 2, 1, 3]
        for g in range(4):
            wt, wk = self.W.acquire()
            wv = wt[:, 0:KC * 256].rearrange("p (c n) -> p c n", c=KC)
            for pos, r in enumerate(order):
                def dst_fn(tb, pos=pos):
                    return QT[:, tb * 4:(tb + 1) * 4, pos, :]
                self.proj_fm(wv, wk, r * 64, 64, self.rope_post(dst_fn, ("ar", "qt"), 0.125))
            self.W.release()
            wt, wk = self.W.acquire()
            wv = wt[:, 0:KC * 256].rearrange("p (c n) -> p c n", c=KC)
            for b_ in range(3):
                dstk = KT0 if b_ == 0 else KT[b_]
                dk = OTK if b_ == 0 else ("ar", "kt", b_)
                self.proj_fm(wv, wk, b_ * 64, 64, self.rope_post(
                    (lambda dstk: lambda tb: dstk[:, tb * 512:(tb + 1) * 512].rearrange("p (a b) -> p a b", a=4))(dstk), dk, 1.0))

            def post_v0(tb, b):
                blk = slice(tb * 512, (tb + 1) * 512)
                P.op("act", lambda e: e.copy(out=V0T[:, blk], in_=self.ps[b][0:64, :]), reads=[("ps", b)], writes=[OTK])
            self.proj_fm(wv, wk, 192, 64, post_v0)
            self.W.release()
            wt, wk = self.W.acquire()
            wv = wt[:, 0:KC * 128].rearrange("p (c n) -> p c n", c=KC)
            for b_ in (1, 2):
                self.proj_tm(wv, wk, (b_ - 1) * 64, VX[b_], ("ar", "vx", b_))
            self.W.release()
            w1s = [None] * 4
            for i in range(2):
                wt_, wk_ = self.W.acquire()
                w1s[i] = (wt_[0:64, 0:2048].rearrange("p (l n) -> p l n", l=16), wk_)
            wt, wks = self.W.acquire()
            def comp(which, src, wt, wks, w1s):
                b = self.psb(6, 8)
                srcv = src.rearrange("p (n s) -> p n s", s=16)
                nmm = 64
                cnt = 0
                for l_ in range(32):
                    w1v, w1k = w1s[which * 2 + l_ // 16]
                    rhs = srcv[:, 0:127, l_] if l_ < 16 else srcv[:, 1:128, l_ - 16]
                    pe_col = wt[0:64, 128 + which * 32 + l_:128 + which * 32 + l_ + 1].to_broadcast([64, 127])
                    for rr, rk in ((rhs, [OTK]), (pe_col, [wks])):
                        P.op("pe", (lambda b, w1v, l_, rr, cnt: lambda e: e.matmul(
                            self.ps[b][:, 0:127], lhsT=w1v[:, l_ % 16, :], rhs=rr, start=(cnt == 0), stop=(cnt == nmm - 1)))(b, w1v, l_, rr, cnt),
                            reads=[w1k] + rk, writes=[("ps", b)])
                        cnt += 1
                u, t_, sg_ = gel
                GK = ("ar", "gel")
                P.op("act", lambda e: e.copy(out=u[:, 0:127], in_=self.ps[b][:, 0:127]), reads=[("ps", b)], writes=[GK])
                P.op("dve", lambda e: e.tensor_tensor(out=t_[:, 0:127], in0=u[:, 0:127], in1=u[:, 0:127], op=ALU.mult), reads=[GK], writes=[GK])
                P.op("dve", lambda e: e.tensor_scalar(out=t_[:, 0:127], in0=t_[:, 0:127], scalar1=0.044715, scalar2=1.0,
                                                      op0=ALU.mult, op1=ALU.add), reads=[GK], writes=[GK])
                P.op("dve", lambda e: e.tensor_tensor(out=t_[:, 0:127], in0=t_[:, 0:127], in1=u[:, 0:127], op=ALU.mult), reads=[GK], writes=[GK])
                P.op("act", lambda e: e.activation(out=sg_[:, 0:127], in_=t_[:, 0:127], func=AF.Sigmoid, scale=1.5957691216057308),
                     reads=[GK], writes=[GK])
                P.op("dve", lambda e: e.tensor_tensor(out=gbf[:, 0:127], in0=u[:, 0:127], in1=sg_[:, 0:127], op=ALU.mult),
                     reads=[GK], writes=[("ar", "gbf")])
                b2 = self.psb(6, 8)
                if which == 0:
                    P.op("pe", lambda e: e.matmul(self.ps[b2][0:64, 0:127], lhsT=wt[:, 0:64], rhs=gbf[:, 0:127], start=True, stop=True),
                         reads=[wks, ("ar", "gbf")], writes=[("ps", b2)])
                    P.op("act", lambda e: e.copy(out=KTc[:, 0:127], in_=self.ps[b2][0:64, 0:127]), reads=[("ps", b2)], writes=[("ar", "ktc")])
                else:
                    P.op("pe", lambda e: e.matmul(self.ps[b2][0:127, 0:64], lhsT=gbf[:, 0:127], rhs=wt[:, 64:128], start=True, stop=True),
                         reads=[wks, ("ar", "gbf")], writes=[("ps", b2)])
                    P.op("act", lambda e: e.copy(out=VXc[0:127, 0:64], in_=self.ps[b2][0:127, 0:64]), reads=[("ps", b2)], writes=[("ar", "vxc")])
            comp(0, KT0, wt, wks, list(w1s))
            self.W.release()
            self.W.release()
            for i in range(2, 4):
                wt_, wk_ = self.W.acquire()
                w1s[i] = (wt_[0:64, 0:2048].rearrange("p (l n) -> p l n", l=16), wk_)
            comp(1, V0T, wt, wks, list(w1s))
            self.W.release()
            self.W.release()
            self.W.release()
            SB = (0, 1, 2)
            rd_s = rd
            rd_br = {2: rg, 1: rd2}
            accs = [acc, acc2]
            accc = [self.rsc[0][0], self.rsc[0][1]]
            accck = [("ar", "qf", 0), ("ar", "t1", 0)]

            def bc4(ap):
                return ap.rearrange("p (o n) -> p o n", o=1).to_broadcast([ap.shape[0], 4, 128])

            def gates_ps(br, qt, g=g):
                tsl = slice(qt * 128, (qt + 1) * 128)
                bg = self.psb(6, 8)
                for pos, r in enumerate(order):
                    row = br * 16 + g * 4 + r
                    P.op("pe", (lambda bg, pos, row: lambda e: e.matmul(
                        self.ps[bg][0:64, pos * 128:(pos + 1) * 128], lhsT=self.identb[0:48, row:row + 1].to_broadcast([48, 64]),
                        rhs=GT[:, tsl], start=True, stop=True))(bg, pos, row),
                        reads=["identb", ("ar", "gt")], writes=[("ps", bg)])
                return bg

            def S1(qt):
                par = qt % 2
                tsl = slice(qt * 128, (qt + 1) * 128)
                qrhs = QT[:, qt, :, :]
                steps = [(KTc[:], [("ar", "ktc")], [(self.identb[:], bc4(mw[:, tsl]), ["identb", ("ar", "mw")])],
                          VXc[:], [("ar", "vxc")])]
                pi = self.ptc % 3
                bo_c = self.attn(qrhs, [("ar", "qt")], steps, den_extra=(e01, tiny, [("ar", "e01")]), bo=3, sbanks=SB)
                bi = self.psb(6, 8)
                ptile = self.pt[pi]
                P.op("pe", lambda e: e.matmul(self.ps[bi][0:32, :], lhsT=ov[:], rhs=ptile[:], start=True, stop=True),
                     reads=[("ar", "ov"), ("ar", "pt", pi)], writes=[("ps", bi)])
                P.op("act", lambda e: e.activation(out=rd_s[:], in_=self.ps[bo_c][64:128, :], func=AF.Ln), reads=[("ps", bo_c)], writes=[("ar", "qf", 1)])
                P.op("act", lambda e: e.activation(out=rd_s[:], in_=rd_s[:], func=AF.Exp, scale=-1.0), reads=[("ar", "qf", 1)], writes=[("ar", "qf", 1)])
                P.op("dve", lambda e: e.tensor_tensor(out=impn[:], in0=self.ps[bi][0:32, :], in1=rd_s[0:32, :], op=ALU.mult),
                     reads=[("ps", bi), ("ar", "qf", 1)], writes=[("ar", "impn")])
                P.op("dve", lambda e: e.tensor_tensor(out=impT[:], in0=impn[:, 0:128], in1=impn[:, 128:256], op=ALU.add),
                     reads=[("ar", "impn")], writes=[("ar", "impT")])
                P.op("dve", lambda e: e.tensor_tensor(out=impT[:], in0=impT[:], in1=impn[:, 256:384], op=ALU.add),
                     reads=[("ar", "impn"), ("ar", "impT")], writes=[("ar", "impT")])
                P.op("dve", lambda e: e.tensor_tensor(out=impT[:], in0=impT[:], in1=impn[:, 384:512], op=ALU.add),
                     reads=[("ar", "impn"), ("ar", "impT")], writes=[("ar", "impT")])
                bg = gates_ps(0, qt)
                ac = accc[par]
                ak = accck[par]
                P.op("dve", lambda e: e.tensor_tensor(out=ac[:], in0=self.ps[bg][0:64, :], in1=rd_s[:], op=ALU.mult),
                     reads=[("ps", bg), ("ar", "qf", 1)], writes=[ak])
                P.op("dve", lambda e: e.tensor_tensor(out=ac[:], in0=self.ps[bo_c][0:64, :], in1=ac[:], op=ALU.mult),
                     reads=[("ps", bo_c), ak], writes=[ak])

            def S2(qt):
                bt = self.psb(6, 8)
                P.op("pe", lambda e: e.matmul(self.ps[bt][:, 0:32], lhsT=impT[:], rhs=self.ident[0:32, 0:32], start=True, stop=True),
                     reads=[("ar", "impT"), "ident"], writes=[("ps", bt)])
                fsl = slice(32 - 2 * qt, 64 - 2 * qt)
                P.op("dve", lambda e: e.tensor_tensor(out=im1[:], in0=self.ps[bt][:, 0:32], in1=kw[:, fsl], op=ALU.mult),
                     reads=[("ps", bt), ("ar", "fw")], writes=[("ar", "im1")])
                P.op("dve", lambda e: e.tensor_tensor(out=im1[:], in0=im1[:], in1=fw[:, fsl], op=ALU.add),
                     reads=[("ar", "im1"), ("ar", "fw")], writes=[("ar", "im1")])
                P.op("dve", lambda e: e.memset(im1[:, 0:1], 1.0e4), reads=[("ar", "im1")], writes=[("ar", "im1")])
                P.op("dve", lambda e: e.max(out=m8[:, 0:8], in_=im1[:]), reads=[("ar", "im1")], writes=[("ar", "m8")])
                P.op("dve", lambda e: e.match_replace(out=im2[:], in_to_replace=m8[:, 0:8], in_values=im1[:], imm_value=-1.0e30),
                     reads=[("ar", "im1"), ("ar", "m8")], writes=[("ar", "im2")])
                P.op("dve", lambda e: e.max(out=m8[:, 8:16], in_=im2[:]), reads=[("ar", "im2")], writes=[("ar", "m8b")])
                P.op("dve", lambda e: e.tensor_scalar(out=seln[:], in0=im1[:], scalar1=m8[:, 15:16], scalar2=None, op0=ALU.is_ge),
                     reads=[("ar", "im1"), ("ar", "m8b")], writes=[("ar", "seln")])
                P.op("dve", lambda e: e.tensor_scalar(out=seln[:], in0=seln[:], scalar1=1.0, scalar2=-NEGM, op0=ALU.subtract, op1=ALU.mult),
                     reads=[("ar", "seln")], writes=[("ar", "seln")])

            def S3(qt):
                par = qt % 2
                bt2 = self.psb(6, 8)
                P.op("pe", lambda e: e.matmul(self.ps[bt2][0:32, 0:128], lhsT=seln[:], rhs=self.ident[:], start=True, stop=True),
                     reads=[("ar", "seln"), "ident"], writes=[("ps", bt2)])
                P.op("act", lambda e: e.copy(out=QTX[64:96, qt, :, :], in_=bc4(self.ps[bt2][0:32, 0:128])),
                     reads=[("ps", bt2)], writes=[("ar", "qsel", par)])

            def fin(bo, br, qt, first):
                par = qt % 2
                rd_m = rd_br[br]
                rmk = ("ar", "t1", 1) if br == 2 else ("ar", "rd_m", br)
                acc = accs[par]
                acck = ("ar", "t2", 1) if par == 0 else ("ar", "acc", par)
                P.op("act", lambda e: e.activation(out=rd_m[:], in_=self.ps[bo][64:128, :], func=AF.Ln), reads=[("ps", bo)], writes=[rmk])
                P.op("act", lambda e: e.activation(out=rd_m[:], in_=rd_m[:], func=AF.Exp, scale=-1.0), reads=[rmk], writes=[rmk])
                bg = gates_ps(br, qt)
                P.op("dve", lambda e: e.tensor_tensor(out=rd_m[:], in0=self.ps[bg][0:64, :], in1=rd_m[:], op=ALU.mult),
                     reads=[("ps", bg), rmk], writes=[rmk])
                P.op("dve", lambda e: e.tensor_tensor(out=rd_m[:], in0=self.ps[bo][0:64, :], in1=rd_m[:], op=ALU.mult),
                     reads=[("ps", bo), rmk], writes=[rmk])
                src = accc[par] if first else acc
                sk = accck[par] if first else acck
                P.op("dve", lambda e: e.tensor_tensor(out=acc[:], in0=src[:], in1=rd_m[:], op=ALU.add),
                     reads=[sk, rmk], writes=[acck])

            def Mw(qt):
                qrhs = QT[:, qt, :, :]
                steps = []
                for kt in range(max(0, qt - 4), qt + 1):
                    masks = []
                    if kt == qt - 4:
                        masks.append((self.identb[:], bc4(bw[:]), ["identb", ("ar", "bw")]))
                    if kt == qt:
                        masks.append((self.identb[:], bc4(cw[:, 384:512]), ["identb", ("ar", "cw")]))
                    steps.append((KT[2][:, kt * 128:(kt + 1) * 128], [("ar", "kt", 2)], masks, VX[2][:, kt, :], [("ar", "vx", 2), ("ar", "vx1", 2)]))
                bo_w = self.attn(qrhs, [("ar", "qt")], steps, bo=4, sbanks=SB)
                fin(bo_w, 2, qt, True)

            def Ms_attn(qt):
                par = qt % 2
                qrhs = QTX[:, qt, :, :]
                steps = []
                for kt in range(0, qt + 1):
                    masks = []
                    if kt == qt:
                        masks.append((self.identb[:], bc4(cw[:, 384:512]), ["identb", ("ar", "cw")]))
                    steps.append((KT1X[:, kt * 128:(kt + 1) * 128], [("ar", "kt", 1), ("ar", "ew")], masks, VX[1][:, kt, :],
                                  [("ar", "vx", 1), ("ar", "vx1", 1)]))
                return self.attn(qrhs, [("ar", "qt"), ("ar", "qsel", par)], steps, bo=5, sbanks=SB)

            def Ms_fin(bo_s, qt):
                tsl = slice(qt * 128, (qt + 1) * 128)
                fin(bo_s, 1, qt, False)
                accq = accs[qt % 2]
                aqk = ("ar", "t2", 1) if qt % 2 == 0 else ("ar", "acc", qt % 2)
                P.op("act", lambda e: e.copy(out=OT[0:64, :, tsl], in_=accq[:, 0:256].rearrange("p (a b) -> p a b", a=2)),
                     reads=[aqk], writes=[OTK])
                P.op("act", lambda e: e.copy(out=OT[64:128, :, tsl], in_=accq[:, 256:512].rearrange("p (a b) -> p a b", a=2)),
                     reads=[aqk], writes=[OTK])

            S1(0)
            S2(0)
            S3(0)
            def M_attn(qt):
                par = qt % 2
                qrhs = QTX[:, qt, :, :]
                steps = []
                for kt in range(max(0, qt - 4), qt + 1):
                    masks = []
                    if kt == qt - 4:
                        masks.append((self.identb[:], bc4(bw[:]), ["identb", ("ar", "bw")]))
                    if kt == qt:
                        masks.append((self.identb[:], bc4(cw[:, 384:512]), ["identb", ("ar", "cw")]))
                    steps.append((KT[2][:, kt * 128:(kt + 1) * 128], [("ar", "kt", 2)], masks, VX[2][:, kt, :],
                                  [("ar", "vx", 2), ("ar", "vx1", 2)], 4))
                nw = len(steps)
                for kt in range(0, qt + 1):
                    masks = []
                    if kt == qt:
                        masks.append((self.identb[:], bc4(cw[:, 384:512]), ["identb", ("ar", "cw")]))
                    steps.append((KT1X[:, kt * 128:(kt + 1) * 128], [("ar", "kt", 1), ("ar", "ew")], masks, VX[1][:, kt, :],
                                  [("ar", "vx", 1), ("ar", "vx1", 1)], 5))
                hooks = {}
                if qt + 1 < NT:
                    hooks.setdefault(nw - 1, []).append(lambda: S2(qt + 1))
                hooks.setdefault(min(nw + 1, len(steps) - 1), []).append(lambda: fin(4, 2, qt, True))
                return steps, hooks

            for qt in range(NT):
                if qt + 1 < NT:
                    S1(qt + 1)
                steps_m, hooks_m = M_attn(qt)
                self.attn_mixed(QT[:, qt, :, :], QTX[:, qt, :, :], qt, steps_m, hooks_m, SB)
                if qt + 1 < NT:
                    S3(qt + 1)
                Ms_fin(5, qt)
            wt, wk = self.W.acquire()
            wv = wt[:, 0:2048].rearrange("p (a n) -> p a n", a=2)
            self.out_proj(wv, wk, 2, OT, OTK)
            self.W.release()
        self.dump("ot", OT[:], [OTK], BF16)
        self.dump("ktc", KTc[:], [("ar", "ktc")], BF16)
        self.dump("vxc", VXc[:], [("ar", "vxc")], BF16)
        self.dump("gt", GT[:], [("ar", "gt")], BF16)
        self.dump("seln", seln[:], [("ar", "seln")])
        self.dump("im1", im1[:], [("ar", "im1")])
        self.dump("m8", m8[:], [("ar", "m8"), ("ar", "m8b")])

def consts():
    import ml_dtypes
    bf = ml_dtypes.bfloat16
    c = {"c_ident": np.eye(128, dtype=np.float32)}
    inv = (10000.0 ** (-np.arange(0, 64, 2, dtype=np.float32) / 64)).astype(np.float32)
    ang = np.arange(S, dtype=np.float32)[:, None] * inv[None, :]
    cos = np.cos(ang).astype(np.float32).T
    sin = np.sin(ang).astype(np.float32).T
    c["c_cos"] = np.ascontiguousarray(np.concatenate([cos, cos], 0))
    c["c_sin"] = np.ascontiguousarray(np.concatenate([sin, sin], 0))
    rot = np.zeros((64, 64), np.float32)
    for d in range(32):
        rot[d + 32, d] = -1.0
        rot[d, d + 32] = 1.0
    c["c_rot"] = rot
    sp = np.arange(128)[:, None]
    cc = np.arange(896)[None, :]
    c["c_cw"] = np.where(sp + 384 <= cc, 0.0, NEGM).astype(bf)
    ii = np.arange(128)[None, :]
    c["c_bw"] = np.where(sp > ii, 0.0, NEGM).astype(bf)
    n = np.arange(128)[:, None]
    t = np.arange(S)[None, :]
    c["c_mw"] = np.where((n * 16 + 31 <= t) & (n < 127), 0.0, NEGM).astype(bf)
    jj = np.arange(32)[:, None]
    c["c_ew"] = (jj == (np.arange(S)[None, :] // 64)).astype(np.float32).astype(bf)
    fw = np.zeros((128, 64), np.float32)
    kw = np.ones((128, 64), np.float32)
    for i in range(128):
        tbr = 1 if i >= 64 else 0
        for cidx in range(64):
            rel = cidx - 32
            if rel == tbr:
                fw[i, cidx] = 2.0e4; kw[i, cidx] = 0.0
            elif rel == tbr - 1:
                fw[i, cidx] = 3.0e4; kw[i, cidx] = 0.0
            elif rel > tbr:
                fw[i, cidx] = -1.0e4 * (cidx + 1); kw[i, cidx] = 0.0
    c["c_fw"] = fw
    c["c_kw"] = kw
    cs = np.arange(127) * 16
    ce = cs + 32
    ss = np.arange(32) * 64
    se = ss + 64
    ovm = np.clip(np.minimum(ce[:, None], se[None, :]) - np.maximum(cs[:, None], ss[None, :]), 0, None) / 32.0
    ovp = np.zeros((128, 32), np.float32)
    ovp[:127] = ovm
    c["c_ov"] = ovp.astype(bf)
    return c


_CACHE = {}


def kernel(**inputs):
    cfg = {}
    b = Builder(cfg)
    nc = b.build()
    names = list(b.dram.keys())
    cs = consts()
    in_maps = []
    x = np.ascontiguousarray(inputs["x"])
    for core in range(8):
        m = {}
        for nm in names:
            if nm == "x":
                m[nm] = x[core]
            elif nm in cs:
                m[nm] = cs[nm]
            else:
                m[nm] = np.ascontiguousarray(inputs[nm])
        in_maps.append(m)
    res = run_bass_kernel_spmd(nc, in_maps, core_ids=list(range(8)))
    return np.stack([r["out"] for r in res.results], axis=0)
```

```python
import numpy as np
import concourse.bass as bass
import concourse.mybir as mybir
from concourse.bass_utils import run_bass_kernel_spmd
from contextlib import ExitStack

F32 = mybir.dt.float32
BF16 = mybir.dt.bfloat16
AF = mybir.ActivationFunctionType
ALU = mybir.AluOpType

S = 2048
D = 1024
DEPTH = 4
DFF = 2816
NT = 16
KC = 8
NB = 4
NJ = 22
EPS = 1e-6
NEGM = -30000.0

ENGS = ("pe", "act", "dve", "pool", "sp")


class Prog:
    def __init__(self, nc):
        self.nc = nc
        self.ops = []
        self.es = ExitStack()

    def sbuf(self, name, shape, dt):
        return self.es.enter_context(self.nc.sbuf_tensor(name, list(shape), dt))

    def psum(self, name, shape, dt):
        return self.es.enter_context(self.nc.psum_tensor(name, list(shape), dt))

    @staticmethod
    def _at(reads, writes):
        reads = tuple(reads)
        for k in tuple(writes) + reads:
            if isinstance(k, tuple) and k and k[0] == "ar":
                return reads + ("AT",)
        return reads

    def op(self, eng, fn, reads=(), writes=()):
        self.ops.append(dict(eng=eng, fn=fn, reads=self._at(reads, writes), writes=tuple(writes), dma=None))

    def dma(self, eng, fn, stream, reads=(), writes=()):
        self.ops.append(dict(eng=eng, fn=fn, reads=self._at(reads, writes), writes=tuple(writes), dma=stream))

    def emit(self, final_wait_streams=()):
        nc = self.nc
        ops = self.ops
        n = len(ops)
        last_w = {}
        readers = {}
        deps = [None] * n
        for i, o in enumerate(ops):
            d = set()
            for k in o["reads"]:
                if k in last_w:
                    d.add(last_w[k])
            for k in o["writes"]:
                if k in last_w:
                    d.add(last_w[k])
                for r in readers.get(k, {}).values():
                    d.add(r)
            d.discard(i)
            if o["eng"] == "pe" and o["dma"] is None:
                d = {j for j in d if not (ops[j]["eng"] == "pe" and ops[j]["dma"] is None)}
            best = {}
            for j in d:
                kk = ("dma", ops[j]["dma"]) if ops[j]["dma"] is not None else ops[j]["eng"]
                if best.get(kk, -1) < j:
                    best[kk] = j
            deps[i] = set(best.values())
            rk = ("dma", o["dma"]) if o["dma"] is not None else o["eng"]
            for k in o["reads"]:
                readers.setdefault(k, {})[rk] = i
            for k in o["writes"]:
                last_w[k] = i
                readers[k] = {}
        target = [False] * n
        for i in range(n):
            for j in deps[i]:
                target[j] = True
        eng_cnt = {e: 0 for e in ENGS}
        stream_cnt = {}
        ev = [None] * n
        for i, o in enumerate(ops):
            if o["dma"] is not None:
                s = o["dma"]
                stream_cnt[s] = stream_cnt.get(s, 0) + 1
                ev[i] = (("dma", s), 16 * stream_cnt[s])
            elif target[i]:
                eng_cnt[o["eng"]] += 1
                ev[i] = (("eng", o["eng"]), eng_cnt[o["eng"]])
        sems = {}
        for e in ENGS:
            sems[("eng", e)] = self.es.enter_context(nc.semaphore("sem_" + e))
        for s in stream_cnt:
            sems[("dma", s)] = self.es.enter_context(nc.semaphore("dsem_%s" % (s,)))
        per_eng = {e: [] for e in ENGS}
        seen = {e: {} for e in ENGS}
        issued = {}
        nwaits = 0
        for i, o in enumerate(ops):
            need = {}
            for j in deps[i]:
                sk, v = ev[j]
                if sk[0] == "dma":
                    v = 16 * issued[sk[1]]
                if need.get(sk, 0) < v:
                    need[sk] = v
            if o["dma"] is not None:
                issued[o["dma"]] = issued.get(o["dma"], 0) + 1
            waits = []
            for sk, v in need.items():
                if seen[o["eng"]].get(sk, 0) >= v:
                    continue
                seen[o["eng"]][sk] = v
                waits.append((sk, v))
            nwaits += len(waits)
            per_eng[o["eng"]].append((i, waits))
        self.stats = dict(n_ops=n, n_waits=nwaits, per_eng={e: len(per_eng[e]) for e in ENGS},
                          n_sems=len(sems), max_cnt=dict(eng_cnt))
        finals = [(("dma", s), 16 * stream_cnt[s]) for s in final_wait_streams]
        block = self.es.enter_context(nc.Block())

        def run(engname, engine):
            for i, waits in per_eng[engname]:
                o = ops[i]
                for sk, v in waits:
                    engine.wait_ge(sems[sk], v)
                ins = o["fn"](engine)
                if ev[i] is not None:
                    sk, v = ev[i]
                    ins.then_inc(sems[sk], 16 if o["dma"] is not None else 1)
            if engname == "sp":
                for sk, v in finals:
                    engine.wait_ge(sems[sk], v)

        @block.tensor
        def _(e):
            run("pe", e)

        @block.scalar
        def _(e):
            run("act", e)

        @block.vector
        def _(e):
            run("dve", e)

        @block.gpsimd
        def _(e):
            run("pool", e)

        @block.sync
        def _(e):
            run("sp", e)

    def close(self):
        self.es.close()


class WStream:
    def __init__(self, P, nslots, slot_elems, eng="pool"):
        self.P = P
        self.n = nslots
        self.slots = [P.sbuf("wslot%d" % i, [128, slot_elems], BF16) for i in range(nslots)]
        self.plan = []
        self.next_issue = 0
        self.next_use = 0
        self.eng = eng

    def add(self, fn):
        self.plan.append(fn)

    def _issue(self):
        if self.next_issue >= len(self.plan):
            return
        i = self.next_issue
        self.next_issue += 1
        s = i % self.n
        for d in self.plan[i](self.slots[s]):
            self.P.dma(self.eng, d, "W%d" % s, writes=[("W", s)])

    def start(self):
        for _ in range(self.n):
            self._issue()

    def acquire(self):
        i = self.next_use
        self.next_use += 1
        s = i % self.n
        return self.slots[s], ("W", s)

    def release(self):
        self._issue()


class Builder:
    def __init__(self, cfg):
        self.cfg = cfg
        nc = bass.Bass("TRN2", target_bir_lowering=False)
        self.nc = nc
        self.P = Prog(nc)
        self.dram = {}
        self.psc = 0
        self.dbg = []
        self.dbg_streams = []

    def din(self, name, shape, dt=F32):
        t = self.nc.dram_tensor(name, list(shape), dt, kind="ExternalInput").ap()
        self.dram[name] = t
        return t

    def setup(self):
        P = self.P
        nc = self.nc
        d = self.din
        d("x", [S, D])
        for nm in ("ffn1_norm", "mix_norm", "ffn2_norm"):
            d(nm, [DEPTH, D])
        d("ffn1_w_gu", [DEPTH, D, 2 * DFF])
        d("ffn1_w_down", [DEPTH, DFF, D])
        d("ffn2_w_gu", [DEPTH, D, 2 * DFF])
        d("ffn2_w_down", [DEPTH, DFF, D])
        d("final_norm", [D])
        d("c_ident", [128, 128])
        d("nsa_w_in", [2, D, 2608]); d("nsa_ck_pe", [2, 32, 64]); d("nsa_ck_w1", [2, 2048, 128]); d("nsa_ck_w2", [2, 128, 64])
        d("nsa_cv_pe", [2, 32, 64]); d("nsa_cv_w1", [2, 2048, 128]); d("nsa_cv_w2", [2, 128, 64]); d("nsa_w_out", [2, D, D])
        d("swa_w_in", [1, D, 1280]); d("swa_sinks", [1, 16]); d("swa_w_out", [1, D, D])
        d("fox_w_in", [1, D, 3088]); d("fox_b_f", [1, 16]); d("fox_w_out", [1, D, D])
        d("c_cos", [64, S]); d("c_sin", [64, S]); d("c_rot", [64, 64])
        d("c_cw", [128, 896], BF16); d("c_bw", [128, 128], BF16)
        d("c_mw", [128, S], BF16); d("c_ew", [32, S], BF16); d("c_fw", [128, 64]); d("c_kw", [128, 64]); d("c_ov", [128, 32], BF16)
        self.out = nc.dram_tensor("out", [S, D], F32, kind="ExternalOutput").ap()

        self.x_fm = P.sbuf("x_fm", [128, KC, S], F32)
        self.h = P.sbuf("h", [128, KC, S], BF16)
        self.ident = P.sbuf("ident", [128, 128], F32)
        self.identb = P.sbuf("identb", [128, 128], BF16)
        self.ones_bf = P.sbuf("ones_bf", [128, 128], BF16)
        self.onesf = P.sbuf("onesf", [128, 1], F32)
        self.fence_t = P.sbuf("fence_t", [128, 8], F32)
        self.gains = P.sbuf("gains", [128, 13, KC], F32)
        self.ps = [P.psum("ps%d" % i, [128, 512], F32) for i in range(8)]
        self.W = WStream(P, 4, 2048)
        self.ARENA = 48000
        self.arena = P.sbuf("arena", [128, self.ARENA], BF16)
        self.ab = 0
        self.a = self.aa([128, 11, S], BF16)
        self.sq = [self.aa([128, KC, 512], BF16)]
        self.rstd = [self.aa([128, 512], F32) for i in range(2)]
        self.sg = [self.aa([128, 512], F32) for i in range(2)]
        self.stage = [self.aa([128, D], F32) for i in range(2)]
        P.op("dve", lambda e: e.memset(self.onesf[:], 1.0), writes=["onesf"])

        P.dma("sp", lambda e: e.dma_start(out=self.ident[:], in_=self.dram["c_ident"]), self.ustream(), writes=["ident"])
        P.op("act", lambda e: e.copy(out=self.identb[:], in_=self.ident[:]), reads=["ident"], writes=["identb"])
        P.op("dve", lambda e: e.memset(self.ones_bf[:], 1.0), writes=["ones_bf"])
        self.graw = P.sbuf("graw", [104, 128], F32)
        for l in range(DEPTH):
            for j, nm in enumerate(("ffn1_norm", "mix_norm", "ffn2_norm")):
                idx = l * 3 + j
                src = self.dram[nm][l, :].rearrange("(c p) -> c p", p=128)
                P.dma("sp", (lambda idx, src: lambda e: e.dma_start(out=self.graw[idx * 8:(idx + 1) * 8, :], in_=src))(idx, src),
                      "graw", writes=["graw"])
        src = self.dram["final_norm"].rearrange("(c p) -> c p", p=128)
        P.dma("sp", lambda e: e.dma_start(out=self.graw[96:104, :], in_=src), "graw", writes=["graw"])
        P.op("pe", lambda e: e.matmul(self.ps[7][:, 0:104], lhsT=self.graw[:], rhs=self.ident[0:104, 0:104],
                                      start=True, stop=True), reads=["graw", "ident"], writes=[("ps", 7)])
        P.op("act", lambda e: e.copy(out=self.gains[:].rearrange("p a c -> p (a c)"), in_=self.ps[7][:, 0:104]),
             reads=[("ps", 7)], writes=["gains"])

    def aa(self, shape, dt):
        esz = 4 if dt == F32 else 2
        nel = 1
        for d_ in shape[1:]:
            nel *= d_
        off = (self.ab + 31) // 32 * 32
        self.ab = off + nel * esz
        assert self.ab <= self.ARENA * 2, ("arena overflow", self.ab)
        v = self.arena[0:shape[0], off // 2:(off + nel * esz) // 2]
        if dt == F32:
            v = v.bitcast(F32)
        if len(shape) == 3:
            v = v.rearrange("p (a b) -> p a b", a=shape[1])
        elif len(shape) == 4:
            v = v.rearrange("p (a b c) -> p a b c", a=shape[1], b=shape[2])
        return v

    def dump(self, name, view, keys, dt=F32):
        if not self.cfg.get("dbg"):
            return
        shp = [view.shape[0], int(np.prod(view.shape[1:]))]
        t = self.nc.dram_tensor("dbg_" + name, shp, dt, kind="ExternalOutput").ap()
        self.dbg.append("dbg_" + name)
        src = view
        if len(view.shape) == 3:
            t = t.rearrange("p (a b) -> p a b", a=view.shape[1])
        self.P.dma("sp", lambda e: e.dma_start(out=t, in_=src), "dbg_" + name, reads=list(keys))
        self.dbg_streams.append("dbg_" + name)

    def fence(self):
        self.P.op("dve", lambda e: e.memset(self.fence_t[:, 0:1], 0.0), writes=["AT"])

    def psb(self, lo=4, hi=8):
        b = lo + (self.psc % (hi - lo))
        self.psc += 1
        return b

    def plan_ffn(self, l, which):
        if self.cfg.get("noffn"):
            return
        wgu = self.dram["ffn%d_w_gu" % which][l].rearrange("(c p) n -> p c n", p=128)
        wdn = self.dram["ffn%d_w_down" % which][l]
        for gr in range(2):
            for jj in range(11):
                j = gr * 11 + jj

                def f(slot, j=j):
                    v = slot[:, 0:2048].rearrange("p (c n) -> p c n", c=KC)
                    return [lambda e: e.dma_start(out=v[:, :, 0:128], in_=wgu[:, :, j * 128:(j + 1) * 128]),
                            lambda e: e.dma_start(out=v[:, :, 128:256], in_=wgu[:, :, DFF + j * 128:DFF + (j + 1) * 128])]
                self.W.add(f)
            for m in range(KC):
                def f(slot, m=m, gr=gr):
                    v = slot[:, 0:11 * 128].rearrange("p (c n) -> p c n", c=11)
                    src = wdn[gr * 1408:(gr + 1) * 1408, m * 128:(m + 1) * 128].rearrange("(c p) n -> p c n", p=128)
                    return [lambda e: e.dma_start(out=v, in_=src)]
                self.W.add(f)

    def load_x(self):
        P = self.P
        x = self.dram["x"]
        for tt in range(NT):
            st = self.stage[tt % 2]
            sk = ("ar", "stage", tt % 2)
            P.dma("sp", (lambda tt, st: lambda e: e.dma_start(out=st[:], in_=x[tt * 128:(tt + 1) * 128, :]))(tt, st),
                  "xin%d" % (tt % 2), writes=[sk])
            for half in range(2):
                b = self.psb()
                for cc in range(4):
                    c = half * 4 + cc
                    P.op("pe", (lambda b, cc, c, st: lambda e: e.matmul(
                        self.ps[b][:, cc * 128:(cc + 1) * 128], lhsT=st[:, c * 128:(c + 1) * 128],
                        rhs=self.ident[:], start=True, stop=True))(b, cc, c, st),
                        reads=[sk, "ident"], writes=[("ps", b)])
                P.op("act", (lambda b, half, tt: lambda e: e.copy(
                    out=self.x_fm[:, half * 4:(half + 1) * 4, tt * 128:(tt + 1) * 128],
                    in_=self.ps[b][:].rearrange("p (c n) -> p c n", c=4)))(b, half, tt),
                    reads=[("ps", b)], writes=[("x", c4, tt // 4) for c4 in range(half * 4, half * 4 + 4)])

    def rmsnorm(self, gidx, out_fp32_inplace=False):
        P = self.P
        for tb in range(NB):
            blk = slice(tb * 512, (tb + 1) * 512)
            sq = self.sq[0]
            sqk = ("ar", "sq")
            rs = self.rstd[tb % 2]
            rk = ("ar", "rstd", tb % 2)
            xkeys = [("x", c, tb) for c in range(KC)]
            P.op("act", (lambda sq, blk: lambda e: e.activation(out=sq[:], in_=self.x_fm[:, :, blk], func=AF.Square))(sq, blk),
                 reads=xkeys, writes=[sqk])
            b = self.psb()
            for c in range(KC):
                P.op("pe", (lambda b, c, sq: lambda e: e.matmul(self.ps[b][:], lhsT=self.ones_bf[:], rhs=sq[:, c, :],
                                                                start=(c == 0), stop=(c == KC - 1)))(b, c, sq),
                     reads=[sqk, "ones_bf"], writes=[("ps", b)])
            P.op("act", (lambda b, rs: lambda e: e.activation(out=rs[:], in_=self.ps[b][:], func=AF.Ln,
                                                              scale=1.0 / D, bias=self.epsb[:]))(b, rs),
                 reads=[("ps", b), "epsb"], writes=[rk])
            P.op("act", (lambda rs: lambda e: e.activation(out=rs[:], in_=rs[:], func=AF.Exp, scale=-0.5))(rs), reads=[rk], writes=[rk])
            for c in range(KC):
                eng = "dve"
                if out_fp32_inplace:
                    dst = self.x_fm[:, c, blk]
                    wk = [("x", c, tb)]
                else:
                    dst = self.h[:, c, blk]
                    wk = [("h", c, tb)]
                P.op(eng, (lambda dst, c, blk, rs: lambda e: e.scalar_tensor_tensor(
                    out=dst, in0=self.x_fm[:, c, blk], scalar=self.gains[:, gidx, c:c + 1], in1=rs[:],
                    op0=ALU.mult, op1=ALU.mult))(dst, c, blk, rs),
                    reads=[("x", c, tb), rk, "gains"], writes=wk)

    def ffn(self, l, which):
        P = self.P
        if self.cfg.get("noffn"):
            return
        if getattr(self, "_prev_phase", None) != "f":
            self.fence()
        self._prev_phase = "f"
        self.rmsnorm(l * 3 + (0 if which == 1 else 2))
        gcount = 0
        for gr in range(2):
            for jj in range(11):
                wt, wk = self.W.acquire()
                wv = wt[:, 0:2048].rearrange("p (c n) -> p c n", c=KC)
                for tb in range(NB):
                    blk = slice(tb * 512, (tb + 1) * 512)
                    bg = (gcount % 2) * 2
                    bu = bg + 1
                    gcount += 1
                    for k in range(KC):
                        P.op("pe", (lambda bg, k, blk, wv: lambda e: e.matmul(
                            self.ps[bg][:], lhsT=wv[:, k, 0:128], rhs=self.h[:, k, blk],
                            start=(k == 0), stop=(k == KC - 1)))(bg, k, blk, wv),
                            reads=[wk, ("h", k, tb)], writes=[("ps", bg)])
                    for k in range(KC):
                        P.op("pe", (lambda bu, k, blk, wv: lambda e: e.matmul(
                            self.ps[bu][:], lhsT=wv[:, k, 128:256], rhs=self.h[:, k, blk],
                            start=(k == 0), stop=(k == KC - 1)))(bu, k, blk, wv),
                            reads=[wk, ("h", k, tb)], writes=[("ps", bu)])
                    sg = self.sg[gcount % 2]
                    sgk = ("ar", "sg", gcount % 2)
                    P.op("act", (lambda sg, bg: lambda e: e.activation(out=sg[:], in_=self.ps[bg][:], func=AF.Silu))(sg, bg),
                         reads=[("ps", bg)], writes=[sgk])
                    P.op("dve", (lambda sg, bu, jj, blk: lambda e: e.tensor_tensor(
                        out=self.a[:, jj, blk], in0=self.ps[bu][:], in1=sg[:], op=ALU.mult))(sg, bu, jj, blk),
                        reads=[("ps", bu), sgk], writes=[("ar", "a", jj, tb)])
                self.W.release()
            for m in range(KC):
                wt, wk = self.W.acquire()
                wv = wt[:, 0:11 * 128].rearrange("p (c n) -> p c n", c=11)
                for tb in range(NB):
                    blk = slice(tb * 512, (tb + 1) * 512)
                    b = self.psb()
                    for jj in range(11):
                        P.op("pe", (lambda b, jj, blk, wv: lambda e: e.matmul(
                            self.ps[b][:], lhsT=wv[:, jj, :], rhs=self.a[:, jj, blk],
                            start=(jj == 0), stop=(jj == 10)))(b, jj, blk, wv),
                            reads=[wk, ("ar", "a", jj, tb)], writes=[("ps", b)])
                    P.op("dve", (lambda b, m, blk: lambda e: e.scalar_tensor_tensor(
                        out=self.x_fm[:, m, blk], in0=self.ps[b][:], scalar=0.5, in1=self.x_fm[:, m, blk],
                        op0=ALU.mult, op1=ALU.add))(b, m, blk),
                        reads=[("ps", b), ("x", m, tb)], writes=[("x", m, tb)])
                self.W.release()

    def store_out(self):
        P = self.P
        self.fence()
        self.rmsnorm(12, out_fp32_inplace=True)
        for tt in range(NT):
            st = self.stage[tt % 2]
            sk = ("ar", "stage", tt % 2)
            for half in range(2):
                b = self.psb()
                for cc in range(4):
                    c = half * 4 + cc
                    P.op("pe", (lambda b, cc, c, tt: lambda e: e.matmul(
                        self.ps[b][:, cc * 128:(cc + 1) * 128], lhsT=self.x_fm[:, c, tt * 128:(tt + 1) * 128],
                        rhs=self.ident[:], start=True, stop=True))(b, cc, c, tt),
                        reads=[("x", c, tt // 4), "ident"], writes=[("ps", b)])
                P.op("act", (lambda b, half, st: lambda e: e.copy(
                    out=st[:, half * 512:(half + 1) * 512], in_=self.ps[b][:]))(b, half, st),
                    reads=[("ps", b)], writes=[sk])
            P.dma("sp", (lambda tt, st: lambda e: e.dma_start(out=self.out[tt * 128:(tt + 1) * 128, :], in_=st[:]))(tt, st),
                  "out%d" % (tt % 2), reads=[sk])

    def build(self):
        P = self.P
        self.setup()
        self.epsb = P.sbuf("epsb", [128, 1], F32)
        P.op("dve", lambda e: e.memset(self.epsb[:], EPS), writes=["epsb"])
        layers = self.cfg.get("layers", list(range(DEPTH)))
        phases = []
        for l in layers:
            phases += [("f", l, 1), ("m", l, 0), ("f", l, 2)]
        phases = phases[:self.cfg.get("nphase", len(phases))]
        for (k, l, w) in phases:
            if k == "f":
                self.plan_ffn(l, w)
            else:
                self.plan_mixer(l)
        self.W.start()
        self.load_x()
        for pi_, (k, l, w) in enumerate(phases):
            if k == "f":
                self.ffn(l, w)
            else:
                self.mixer(l)
            if self.cfg.get("dbg"):
                self.dump("px%d" % pi_, self.x_fm[:, 0, :], [("x", 0, t_) for t_ in range(NB)])
        self.store_out()
        P.emit(final_wait_streams=["out0", "out1"] + self.dbg_streams)
        P.close()
        return self.nc

    def plan_mixer(self, l):
        if self.cfg.get("nomix"):
            return
        kind = l % 3
        j = l // 3
        W = self.W
        if kind == 2:
            win = self.dram["fox_w_in"][j].rearrange("(c p) n -> p c n", p=128)
            wout = self.dram["fox_w_out"][j]

            def f(slot):
                v = slot[:, 0:KC * 16].rearrange("p (c n) -> p c n", c=KC)
                return [lambda e: e.dma_start(out=v, in_=win[:, :, 3072:3088])]
            W.add(f)
            for g in range(8):
                for part in range(3):
                    def f(slot, g=g, part=part):
                        v = slot[:, 0:KC * 128].rearrange("p (c n) -> p c n", c=KC)
                        c0 = part * 1024 + g * 128
                        return [lambda e: e.dma_start(out=v, in_=win[:, :, c0:c0 + 128])]
                    W.add(f)

                def f(slot, g=g):
                    v = slot[:, 0:1024]
                    return [lambda e: e.dma_start(out=v, in_=wout[g * 128:(g + 1) * 128, :])]
                W.add(f)
        elif kind == 1:
            win = self.dram["swa_w_in"][j].rearrange("(c p) n -> p c n", p=128)
            wout = self.dram["swa_w_out"][j]
            for g in range(4):
                kv = g // 2

                def f(slot, g=g):
                    v = slot[:, 0:KC * 256].rearrange("p (c n) -> p c n", c=KC)
                    return [lambda e: e.dma_start(out=v, in_=win[:, :, g * 256:(g + 1) * 256])]
                W.add(f)

                def f(slot, kv=kv):
                    v = slot[:, 0:KC * 128].rearrange("p (c n) -> p c n", c=KC)
                    return [lambda e: e.dma_start(out=v[:, :, 0:64], in_=win[:, :, 1024 + kv * 64:1024 + (kv + 1) * 64]),
                            lambda e: e.dma_start(out=v[:, :, 64:128], in_=win[:, :, 1152 + kv * 64:1152 + (kv + 1) * 64])]
                W.add(f)

                def f(slot, g=g):
                    v = slot[:, 0:2048].rearrange("p (a n) -> p a n", a=2)
                    src = wout[g * 256:(g + 1) * 256, :].rearrange("(a p) n -> p a n", p=128)
                    return [lambda e: e.dma_start(out=v, in_=src)]
                W.add(f)
        else:
            self.plan_nsa(l)

    def mixer(self, l):
        if self.cfg.get("nomix"):
            return
        kind = l % 3
        self._prev_phase = "m"
        self.fence()
        if self.cfg.get("dbg") and l == 1:
            self.dump("r_x0", self.x_fm[:, 0, :], [("x", 0, t_) for t_ in range(NB)])
        self.rmsnorm(l * 3 + 1)
        if self.cfg.get("dbg") and l == 1:
            self.dump("r_h0", self.h[:, 0, :], [("h", 0, t_) for t_ in range(NB)], BF16)
            self.dump("r_rstd0", self.rstd[0][:], [("ar", "rstd", 0)])
            self.dump("r_rstd1", self.rstd[1][:], [("ar", "rstd", 1)])
            self.dump("r_sq", self.sq[0][:], [("ar", "sq")], BF16)
            self.dump("r_gains", self.gains[:], ["gains"])
            self.dump("r_epsb", self.epsb[:], ["epsb"])
        self.fence()
        self.ab = 0
        self.scc = 0
        if self.cfg.get("mixstub") and l == 1:
            n = {0: 37, 1: 12, 2: 33}[kind]
            for _ in range(n):
                wt, wk = self.W.acquire()
                self.P.op("dve", (lambda wt: lambda e: e.tensor_copy(out=self.fence_t[:, 1:2], in_=wt[:, 0:1]))(wt), reads=[wk], writes=["fdummy"])
                self.W.release()
            return
        if kind == 2:
            self.fox(l)
        elif kind == 1:
            self.swa(l)
        else:
            self.nsa(l)

    def ustream(self):
        self.usc = getattr(self, "usc", 0) + 1
        return "u%d" % self.usc

    def cload(self, name, view, key):
        self.P.dma("sp", lambda e: e.dma_start(out=view, in_=self.dram[name]), self.ustream(), writes=[key])

    def proj_fm(self, wv, wk, c0, ncols, post):
        P = self.P
        for tb in range(NB):
            blk = slice(tb * 512, (tb + 1) * 512)
            b = self.psb(6, 8)
            for k in range(KC):
                P.op("pe", (lambda b, k, blk: lambda e: e.matmul(
                    self.ps[b][0:ncols, :], lhsT=wv[:, k, c0:c0 + ncols], rhs=self.h[:, k, blk],
                    start=(k == 0), stop=(k == KC - 1)))(b, k, blk),
                    reads=[wk, ("h", k, tb)], writes=[("ps", b)])
            post(tb, b)

    def proj_tm(self, wv, wk, c0, vx, vkey):
        P = self.P
        for half in range(2):
            b = self.psb(6, 8)
            for t8 in range(8):
                tt = half * 8 + t8
                for k in range(KC):
                    P.op("pe", (lambda b, k, tt, t8: lambda e: e.matmul(
                        self.ps[b][:, t8 * 64:(t8 + 1) * 64], lhsT=self.h[:, k, tt * 128:(tt + 1) * 128],
                        rhs=wv[:, k, c0:c0 + 64], start=(k == 0), stop=(k == KC - 1)))(b, k, tt, t8),
                        reads=[wk, ("h", k, tt // 4)], writes=[("ps", b)])
            P.op("act", (lambda b, half: lambda e: e.copy(
                out=vx[:, half * 8:(half + 1) * 8, 0:64],
                in_=self.ps[b][:].rearrange("p (t n) -> p t n", t=8)))(b, half),
                reads=[("ps", b)], writes=[vkey])

    def rope_post(self, dst_fn, dkey, scale):
        P = self.P
        cosT, sinT, rotm = self.cosT, self.sinT, self.rotm

        def post(tb, b):
            blk = slice(tb * 512, (tb + 1) * 512)
            i = self.scc % len(self.rsc)
            self.scc += 1
            qf, t1, t2 = self.rsc[i]
            kq, k1, k2 = ("ar", "qf", i), ("ar", "t1", i), ("ar", "t2", i)
            P.op("act", lambda e: e.mul(out=qf[:], in_=self.ps[b][0:64, :], mul=scale), reads=[("ps", b)], writes=[kq])
            b2 = self.psb(6, 8)
            P.op("pe", lambda e: e.matmul(self.ps[b2][0:64, :], lhsT=rotm[:], rhs=qf[:], start=True, stop=True),
                 reads=[kq, ("ar", "rotm")], writes=[("ps", b2)])
            P.op("dve", lambda e: e.tensor_tensor(out=t1[:], in0=qf[:], in1=cosT[:, blk], op=ALU.mult),
                 reads=[kq, ("ar", "rope")], writes=[k1])
            P.op("dve", lambda e: e.tensor_tensor(out=t2[:], in0=self.ps[b2][0:64, :], in1=sinT[:, blk], op=ALU.mult),
                 reads=[("ps", b2), ("ar", "rope")], writes=[k2])
            dst = dst_fn(tb)
            t1v = t1[:].rearrange("p (a b) -> p a b", a=4)
            t2v = t2[:].rearrange("p (a b) -> p a b", a=4)
            P.op("dve", lambda e: e.tensor_tensor(out=dst, in0=t1v, in1=t2v, op=ALU.add),
                 reads=[k1, k2], writes=[dkey])
        return post

    def setup_rope(self):
        self.cosT = self.aa([64, S], F32)
        self.sinT = self.aa([64, S], F32)
        self.rotm = self.aa([64, 64], F32)
        self.cload("c_cos", self.cosT[:], ("ar", "rope"))
        self.cload("c_sin", self.sinT[:], ("ar", "rope"))
        self.cload("c_rot", self.rotm[:], ("ar", "rotm"))
        self.rsc = [tuple(self.aa([64, 512], F32) for _ in range(3)) for _ in range(2)]

    def attn(self, qrhs, qkeys, steps, den_extra=None, n=512, bo=None, sbanks=(0, 1, 2, 3)):
        P = self.P
        if bo is None:
            bo = 4 + (self.oc % 2)
            self.oc += 1
        ns = len(steps)
        sbank = [None] * ns

        def scores(si):
            kT, kkeys, masks, vap, vkeys = steps[si]
            bs = sbanks[self.sc % len(sbanks)]
            self.sc += 1
            sbank[si] = bs
            P.op("pe", lambda e: e.matmul(self.ps[bs][:, 0:n], lhsT=kT, rhs=qrhs, start=True, stop=(len(masks) == 0)),
                 reads=list(kkeys) + list(qkeys), writes=[("ps", bs)])
            for mi, (ml, mr, mk) in enumerate(masks):
                P.op("pe", (lambda ml, mr, last: lambda e: e.matmul(self.ps[bs][:, 0:n], lhsT=ml, rhs=mr, start=False,
                                                                    stop=last))(ml, mr, mi == len(masks) - 1),
                     reads=list(mk), writes=[("ps", bs)])

        def expv(si):
            kT, kkeys, masks, vap, vkeys = steps[si]
            bs = sbank[si]
            pi = self.ptc % 3
            self.ptc += 1
            pt = self.pt[pi]
            pk = ("ar", "pt", pi)
            P.op("act", lambda e: e.activation(out=pt[:, 0:n], in_=self.ps[bs][:, 0:n], func=AF.Exp),
                 reads=[("ps", bs)], writes=[pk])
            last = (si == ns - 1) and den_extra is None
            P.op("pe", lambda e: e.matmul(self.ps[bo][:, 0:n], lhsT=vap, rhs=pt[:, 0:n], start=(si == 0), stop=last),
                 reads=list(vkeys) + [pk], writes=[("ps", bo)])
            for _ in range(self.cfg.get("dummy", 0)):
                P.op("pe", lambda e: e.matmul(self.ps[bs][:, 0:n], lhsT=vap, rhs=pt[:, 0:n], start=True, stop=True),
                     reads=list(vkeys) + [pk], writes=[("ps", bs)])

        la = min(2, len(sbanks) - 1)
        for si in range(min(la, ns)):
            scores(si)
        for si in range(ns):
            if si + la < ns:
                scores(si + la)
            expv(si)
        if den_extra is not None:
            dl, dr, dk = den_extra
            P.op("pe", lambda e: e.matmul(self.ps[bo][:, 0:n], lhsT=dl, rhs=dr, start=False, stop=True),
                 reads=list(dk), writes=[("ps", bo)])
        return bo

    def out_proj(self, wv, wk, npairs, ot, okey):
        P = self.P
        for m in range(KC):
            for tb in range(NB):
                blk = slice(tb * 512, (tb + 1) * 512)
                b = self.psb(6, 8)
                for p in range(npairs):
                    P.op("pe", (lambda b, p, m, blk: lambda e: e.matmul(
                        self.ps[b][:], lhsT=wv[:, p, m * 128:(m + 1) * 128], rhs=ot[:, p, blk],
                        start=(p == 0), stop=(p == npairs - 1)))(b, p, m, blk),
                        reads=[wk, okey], writes=[("ps", b)])
                P.op("dve", (lambda b, m, blk: lambda e: e.tensor_tensor(
                    out=self.x_fm[:, m, blk], in0=self.ps[b][:], in1=self.x_fm[:, m, blk], op=ALU.add))(b, m, blk),
                    reads=[("ps", b), ("x", m, tb)], writes=[("x", m, tb)])

    def mix_common(self):
        self.sc = 0
        self.oc = 0
        self.ptc = 0
        self.pt = [self.aa([128, 512], BF16) for _ in range(3)]
        self.rd = [self.aa([64, 512], F32) for _ in range(2)]
        self.o32 = [self.aa([64, 512], F32) for _ in range(2)]
        self.fc = 0

    def finalize_simple(self, bo, n, dsts):
        P = self.P
        i = self.fc % 2
        self.fc += 1
        rd, o32 = self.rd[i], self.o32[i]
        kr, ko = ("ar", "rd", i), ("ar", "o32", i)
        P.op("act", lambda e: e.activation(out=rd[:, 0:n], in_=self.ps[bo][64:128, 0:n], func=AF.Ln), reads=[("ps", bo)], writes=[kr])
        P.op("act", lambda e: e.activation(out=rd[:, 0:n], in_=rd[:, 0:n], func=AF.Exp, scale=-1.0), reads=[kr], writes=[kr])
        P.op("dve", lambda e: e.tensor_tensor(out=o32[:, 0:n], in0=self.ps[bo][0:64, 0:n], in1=rd[:, 0:n], op=ALU.mult),
             reads=[("ps", bo), kr], writes=[ko])
        for (dst, c0, nc_, dk) in dsts:
            src = o32[:, c0:c0 + nc_]
            if len(dst.shape) == 3:
                src = src.rearrange("p (a b) -> p a b", a=dst.shape[1])
            P.op("act", (lambda dst, src: lambda e: e.copy(out=dst, in_=src))(dst, src), reads=[ko], writes=[dk])

    def fox(self, l):
        P = self.P
        j = l // 3
        self.mix_common()
        cw = self.aa([128, 896], BF16)
        self.cload("c_cw", cw[:], ("ar", "cw"))
        QT = [self.aa([128, S], BF16) for _ in range(2)]
        KT = [self.aa([128, S], BF16) for _ in range(2)]
        VX = [self.aa([128, NT, 128], BF16) for _ in range(2)]
        OT = self.aa([128, 1, S], BF16)
        spz = self.aa([16, S], F32)
        cn = self.aa([16, S], F32)
        zer = self.aa([16, 512], F32)
        csp = self.aa([16, 6, S], BF16)
        negb = self.aa([16, 1], F32)
        for hh in range(2):
            P.op("dve", (lambda hh: lambda e: e.memset(QT[hh][64:128, :], 0.0))(hh), writes=[("ar", "qaux", hh)])
            P.op("dve", (lambda hh: lambda e: e.memset(KT[hh][64:128, :], 0.0))(hh), writes=[("ar", "kaux", hh)])
            P.op("dve", (lambda hh: lambda e: e.memset(QT[hh][64:70, :], 1.0))(hh), writes=[("ar", "qaux", hh)])
            P.op("dve", (lambda hh: lambda e: e.memset(KT[hh][64:70, :], 1.0))(hh), writes=[("ar", "kaux", hh)])
            P.op("pool", (lambda hh: lambda e: e.memset(VX[hh][:, :, 64:128], 1.0))(hh), writes=[("ar", "vx1", hh)])
        P.op("dve", lambda e: e.memset(zer[:], 0.0), writes=[("ar", "zer")])
        bsrc = self.dram["fox_b_f"][j, :].rearrange("(p o) -> p o", o=1)
        P.dma("sp", lambda e: e.dma_start(out=negb[:], in_=bsrc), self.ustream(), writes=[("ar", "negb")])
        P.op("act", lambda e: e.mul(out=negb[:], in_=negb[:], mul=-1.0), reads=[("ar", "negb")], writes=[("ar", "negb")])
        wt, wk = self.W.acquire()
        wv = wt[:, 0:KC * 16].rearrange("p (c n) -> p c n", c=KC)

        def post_gate(tb, b):
            blk = slice(tb * 512, (tb + 1) * 512)
            P.op("act", lambda e: e.activation(out=spz[:, blk], in_=self.ps[b][0:16, :], func=AF.Exp, scale=-1.0, bias=negb[:]),
                 reads=[("ps", b), ("ar", "negb")], writes=[("ar", "spz", tb)])
            P.op("act", lambda e: e.activation(out=spz[:, blk], in_=spz[:, blk], func=AF.Ln, bias=self.onesf[0:16, :]),
                 reads=[("ar", "spz", tb), "onesf"], writes=[("ar", "spz", tb)])
            init = 0.0 if tb == 0 else cn[:, tb * 512 - 1:tb * 512]
            P.op("dve", lambda e: e.tensor_tensor_scan(out=cn[:, blk], data0=zer[:], data1=spz[:, blk], initial=init,
                                                       op0=ALU.add, op1=ALU.add),
                 reads=[("ar", "spz", tb), ("ar", "zer"), ("ar", "cn")], writes=[("ar", "cn")])
        self.proj_fm(wv, wk, 0, 16, post_gate)
        self.W.release()
        r = spz
        P.op("dve", lambda e: e.tensor_copy(out=csp[:, 0, :], in_=cn[:]), reads=[("ar", "cn")], writes=[("ar", "csp")])
        P.op("dve", lambda e: e.tensor_tensor(out=r[:], in0=cn[:], in1=csp[:, 0, :], op=ALU.subtract),
             reads=[("ar", "cn"), ("ar", "csp")] + [("ar", "spz", t) for t in range(4)], writes=[("ar", "r")])
        P.op("dve", lambda e: e.tensor_copy(out=csp[:, 1, :], in_=r[:]), reads=[("ar", "r")], writes=[("ar", "csp")])
        P.op("dve", lambda e: e.tensor_tensor(out=r[:], in0=r[:], in1=csp[:, 1, :], op=ALU.subtract),
             reads=[("ar", "r"), ("ar", "csp")], writes=[("ar", "r")])
        P.op("dve", lambda e: e.tensor_copy(out=csp[:, 2, :], in_=r[:]), reads=[("ar", "r")], writes=[("ar", "csp")])
        P.op("dve", lambda e: e.tensor_scalar(out=csp[:, 3:6, :], in0=csp[:, 0:3, :], scalar1=-1.0, scalar2=None, op0=ALU.mult),
             reads=[("ar", "csp")], writes=[("ar", "csp")])
        for g in range(8):
            for part in range(3):
                wt, wk = self.W.acquire()
                wv = wt[:, 0:KC * 128].rearrange("p (c n) -> p c n", c=KC)
                for hh in range(2):
                    if part == 0:
                        def post(tb, b, hh=hh):
                            blk = slice(tb * 512, (tb + 1) * 512)
                            P.op("act", lambda e: e.mul(out=QT[hh][0:64, blk], in_=self.ps[b][0:64, :], mul=0.125),
                                 reads=[("ps", b)], writes=[("ar", "qt", hh)])
                        self.proj_fm(wv, wk, hh * 64, 64, post)
                    elif part == 1:
                        def post(tb, b, hh=hh):
                            blk = slice(tb * 512, (tb + 1) * 512)
                            P.op("act", lambda e: e.copy(out=KT[hh][0:64, blk], in_=self.ps[b][0:64, :]),
                                 reads=[("ps", b)], writes=[("ar", "kt", hh)])
                        self.proj_fm(wv, wk, hh * 64, 64, post)
                    else:
                        self.proj_tm(wv, wk, hh * 64, VX[hh], ("ar", "vx", hh))
                self.W.release()
            for hh in range(2):
                hd = g * 2 + hh
                for i3 in range(3):
                    P.dma("sp", (lambda hh, hd, i3: lambda e: e.dma_start(out=QT[hh][64 + i3:65 + i3, :], in_=csp[hd:hd + 1, 3 + i3, :]))(hh, hd, i3),
                          "aux", reads=[("ar", "csp")], writes=[("ar", "qaux", hh)])
                    P.dma("sp", (lambda hh, hd, i3: lambda e: e.dma_start(out=KT[hh][67 + i3:68 + i3, :], in_=csp[hd:hd + 1, i3, :]))(hh, hd, i3),
                          "aux", reads=[("ar", "csp")], writes=[("ar", "kaux", hh)])
            for hh in range(2):
                for qb in range(NB):
                    blk = slice(qb * 512, (qb + 1) * 512)
                    steps = []
                    for kt in range(4 * qb + 4):
                        masks = []
                        m = kt - 4 * qb
                        if m >= 0:
                            masks.append((self.identb[:], cw[:, 384 - m * 128:896 - m * 128], ["identb", ("ar", "cw")]))
                        steps.append((KT[hh][0:70, kt * 128:(kt + 1) * 128], [("ar", "kt", hh), ("ar", "kaux", hh)], masks,
                                      VX[hh][:, kt, :], [("ar", "vx", hh), ("ar", "vx1", hh)]))
                    bo = self.attn(QT[hh][0:70, blk], [("ar", "qt", hh), ("ar", "qaux", hh)], steps)
                    self.finalize_simple(bo, 512, [(OT[hh * 64:(hh + 1) * 64, 0, blk], 0, 512, ("ar", "ot"))])
            wt, wk = self.W.acquire()
            wv = wt[:, 0:1024].rearrange("p (a n) -> p a n", a=1)
            self.out_proj(wv, wk, 1, OT, ("ar", "ot"))
            self.W.release()
        self.dump("cn", cn[:], [("ar", "cn")])
        self.dump("spz", spz[:], [("ar", "r")])
        self.dump("qt", QT[0][:], [("ar", "qt", 0), ("ar", "qaux", 0)], BF16)
        self.dump("kt", KT[0][:], [("ar", "kt", 0), ("ar", "kaux", 0)], BF16)
        self.dump("vx", VX[0][:], [("ar", "vx", 0), ("ar", "vx1", 0)], BF16)
        self.dump("ot", OT[:], [("ar", "ot")], BF16)
        self.dump("pt", self.pt[0][:], [("ar", "pt", 0)], BF16)
        self.dump("rd", self.rd[0][:], [("ar", "rd", 0)])
        self.dump("o32", self.o32[0][:], [("ar", "o32", 0)])

    def swa(self, l):
        P = self.P
        self.mix_common()
        self.setup_rope()
        cw = self.aa([128, 896], BF16)
        self.cload("c_cw", cw[:], ("ar", "cw"))
        bw = self.aa([128, 128], BF16)
        self.cload("c_bw", bw[:], ("ar", "bw"))
        QT = self.aa([64, NT, 4, 128], BF16)
        KT = self.aa([64, S], BF16)
        VX = self.aa([128, NT, 128], BF16)
        OT = self.aa([128, 2, S], BF16)
        esk32 = self.aa([32, 16], F32)
        e0132 = self.aa([32, 128], F32)
        esk = esk32[0:1, :]
        e01 = e0132[0:1, :]
        P.op("pool", lambda e: e.memset(VX[:, :, 64:128], 1.0), writes=[("ar", "vx1")])
        P.op("dve", lambda e: e.memset(e0132[:], 0.0), writes=[("ar", "e01")])
        P.op("dve", lambda e: e.memset(e0132[0:1, 64:128], 1.0), writes=[("ar", "e01")])
        P.op("dve", lambda e: e.memset(esk32[:], 0.0), writes=[("ar", "esk")])
        P.dma("sp", lambda e: e.dma_start(out=esk32[0:1, :], in_=self.dram["swa_sinks"][0:1, :]), self.ustream(), writes=[("ar", "esk")])
        P.op("act", lambda e: e.activation(out=esk32[0:1, :], in_=esk32[0:1, :], func=AF.Exp), reads=[("ar", "esk")], writes=[("ar", "esk")])
        order = [0, 2, 1, 3]
        for g in range(4):
            wt, wk = self.W.acquire()
            wv = wt[:, 0:KC * 256].rearrange("p (c n) -> p c n", c=KC)
            for pos, r in enumerate(order):
                def dst_fn(tb, pos=pos):
                    return QT[:, tb * 4:(tb + 1) * 4, pos, :]
                self.proj_fm(wv, wk, r * 64, 64, self.rope_post(dst_fn, ("ar", "qt"), 0.125))
            self.W.release()
            wt, wk = self.W.acquire()
            wv = wt[:, 0:KC * 128].rearrange("p (c n) -> p c n", c=KC)
            self.proj_fm(wv, wk, 0, 64, self.rope_post(lambda tb: KT[:, tb * 512:(tb + 1) * 512].rearrange("p (a b) -> p a b", a=4),
                                                       ("ar", "kt"), 1.0))
            self.proj_tm(wv, wk, 64, VX, ("ar", "vx"))
            self.W.release()
            for qt in range(NT):
                steps = []
                for kt in (qt - 1, qt):
                    if kt < 0:
                        continue
                    mt = cw[:, 384:512] if kt == qt else bw[:]
                    mr = mt.rearrange("p (o n) -> p o n", o=1).to_broadcast([128, 4, 128])
                    steps.append((KT[:, kt * 128:(kt + 1) * 128], [("ar", "kt")],
                                  [(self.identb[:], mr, ["identb", ("ar", "cw"), ("ar", "bw")])],
                                  VX[:, kt, :], [("ar", "vx"), ("ar", "vx1")]))
                hd0 = g * 4
                sk = self.aa_sinkrow(esk, hd0, order)
                bo = self.attn(QT[:, qt, :, :], [("ar", "qt")], steps, den_extra=(e01, sk, [("ar", "e01"), ("ar", "esk")]))
                tsl = slice(qt * 128, (qt + 1) * 128)
                self.finalize_simple(bo, 512, [(OT[0:64, :, tsl], 0, 256, ("ar", "ot")),
                                               (OT[64:128, :, tsl], 256, 256, ("ar", "ot"))])
            wt, wk = self.W.acquire()
            wv = wt[:, 0:2048].rearrange("p (a n) -> p a n", a=2)
            self.out_proj(wv, wk, 2, OT, ("ar", "ot"))
            self.W.release()
        self.dump("s_qt", QT[:].rearrange("p a b c -> p (a b c)"), [("ar", "qt")], BF16)
        self.dump("s_kt", KT[:], [("ar", "kt")], BF16)
        self.dump("s_vx", VX[:], [("ar", "vx"), ("ar", "vx1")], BF16)
        self.dump("s_ot", OT[:], [("ar", "ot")], BF16)
        self.dump("s_esk", esk32[:], [("ar", "esk")])
        self.dump("s_e01", e0132[:], [("ar", "e01")])
        self.dump("s_rd", self.rd[0][:], [("ar", "rd", 0)])
        self.dump("s_o32", self.o32[0][:], [("ar", "o32", 0)])
        self.dump("s_pt", self.pt[0][:], [("ar", "pt", 0)], BF16)
        self.dump("s_h", self.h[:], [("h", c_, t_) for c_ in range(KC) for t_ in range(NB)], BF16)
        self.dump("s_cos", self.cosT[:], [("ar", "rope")])
        self.dump("s_cw", cw[:], [("ar", "cw")], BF16)
        self.dump("s_bw", bw[:], [("ar", "bw")], BF16)

    def aa_sinkrow(self, esk, hd0, order):
        v = esk[:, hd0:hd0 + 4].rearrange("p (a b) -> p b a", b=2)
        return v.rearrange("p b (a o) -> p b a o", o=1).to_broadcast([1, 2, 2, 128])

    def plan_nsa(self, l):
        j = l // 3
        W = self.W
        win = self.dram["nsa_w_in"][j].rearrange("(c p) n -> p c n", p=128)
        wout = self.dram["nsa_w_out"][j]

        def f(slot):
            v = slot[:, 0:KC * 48].rearrange("p (c n) -> p c n", c=KC)
            return [lambda e: e.dma_start(out=v, in_=win[:, :, 2560:2608])]
        W.add(f)
        for g in range(4):
            def kvcol(b, kvi, g=g):
                return 1024 + ((b * 2 + kvi) * 4 + g) * 64

            def f(slot, g=g):
                v = slot[:, 0:KC * 256].rearrange("p (c n) -> p c n", c=KC)
                return [lambda e: e.dma_start(out=v, in_=win[:, :, g * 256:(g + 1) * 256])]
            W.add(f)

            def f(slot, kvcol=kvcol):
                v = slot[:, 0:KC * 256].rearrange("p (c n) -> p c n", c=KC)
                cols = [kvcol(0, 0), kvcol(1, 0), kvcol(2, 0), kvcol(0, 1)]
                return [(lambda i, c0: lambda e: e.dma_start(out=v[:, :, i * 64:(i + 1) * 64], in_=win[:, :, c0:c0 + 64]))(i, c0)
                        for i, c0 in enumerate(cols)]
            W.add(f)

            def f(slot, kvcol=kvcol):
                v = slot[:, 0:KC * 128].rearrange("p (c n) -> p c n", c=KC)
                cols = [kvcol(1, 1), kvcol(2, 1)]
                return [(lambda i, c0: lambda e: e.dma_start(out=v[:, :, i * 64:(i + 1) * 64], in_=win[:, :, c0:c0 + 64]))(i, c0)
                        for i, c0 in enumerate(cols)]
            W.add(f)
            def addw1(nm):
                w1 = self.dram[nm][j].rearrange("(l d) n -> d l n", d=64)
                for half in range(2):
                    def f(slot, w1=w1, half=half):
                        v = slot[0:64, 0:2048].rearrange("p (l n) -> p l n", l=16)
                        return [lambda e: e.dma_start(out=v, in_=w1[:, half * 16:(half + 1) * 16, :])]
                    W.add(f)
            addw1("nsa_ck_w1")

            def f(slot):
                return [lambda e: e.dma_start(out=slot[:, 0:64], in_=self.dram["nsa_ck_w2"][j]),
                        lambda e: e.dma_start(out=slot[:, 64:128], in_=self.dram["nsa_cv_w2"][j]),
                        lambda e: e.dma_start(out=slot[0:64, 128:160], in_=self.dram["nsa_ck_pe"][j].rearrange("l d -> d l"),
                                              allow_slow_non_contiguous=True),
                        lambda e: e.dma_start(out=slot[0:64, 160:192], in_=self.dram["nsa_cv_pe"][j].rearrange("l d -> d l"),
                                              allow_slow_non_contiguous=True)]
            W.add(f)
            addw1("nsa_cv_w1")

            def f(slot, g=g):
                v = slot[:, 0:2048].rearrange("p (a n) -> p a n", a=2)
                src = wout[g * 256:(g + 1) * 256, :].rearrange("(a p) n -> p a n", p=128)
                return [lambda e: e.dma_start(out=v, in_=src)]
            W.add(f)

    def nsa(self, l):
        P = self.P
        self.sc = 0
        self.oc = 0
        self.ptc = 0
        self.fc = 0
        self.pt = [self.aa([128, 512], BF16) for _ in range(3)]
        self.cosT = self.aa([64, S], F32)
        self.sinT = self.aa([64, S], F32)
        self.rotm = self.aa([64, 64], F32)
        self.cload("c_cos", self.cosT[:], ("ar", "rope"))
        self.cload("c_sin", self.sinT[:], ("ar", "rope"))
        self.cload("c_rot", self.rotm[:], ("ar", "rotm"))
        self.rsc = [tuple(self.aa([64, 512], F32) for _ in range(3))]
        cw = self.aa([128, 896], BF16); self.cload("c_cw", cw[:], ("ar", "cw"))
        bw = self.aa([128, 128], BF16); self.cload("c_bw", bw[:], ("ar", "bw"))
        mw = self.aa([128, S], BF16); self.cload("c_mw", mw[:], ("ar", "mw"))
        fw = self.aa([128, 64], F32); self.cload("c_fw", fw[:], ("ar", "fw"))
        kw = self.aa([128, 64], F32); self.cload("c_kw", kw[:], ("ar", "fw"))
        ov = self.aa([128, 32], BF16); self.cload("c_ov", ov[:], ("ar", "ov"))
        QTX = self.aa([96, NT, 4, 128], BF16)
        QT = QTX[0:64]
        KT1X = self.aa([96, S], BF16)
        self.cload("c_ew", KT1X[64:96, :], ("ar", "ew"))
        KT = [None, KT1X[0:64, :], self.aa([64, S], BF16)]
        VX = [None, self.aa([128, NT, 128], BF16), self.aa([128, NT, 128], BF16)]
        OT = self.aa([128, 2, S], BF16)
        KT0 = OT[0:64, 0, :]
        V0T = OT[0:64, 1, :]
        GT64 = self.aa([64, S], BF16)
        GT = GT64[0:48, :]
        KTc = self.aa([64, 128], BF16)
        VXc = self.aa([128, 128], BF16)
        gel = [self.aa([128, 128], F32) for _ in range(3)]
        gbf = self.aa([128, 128], BF16)
        rd = self.aa([64, 512], F32)
        rg = self.aa([64, 512], F32)
        acc = self.aa([64, 512], F32)
        acc2 = self.aa([64, 512], F32)
        rd2 = self.aa([64, 512], F32)
        self.rsc = [self.rsc[0], (rd, rg, acc)]
        impn = self.aa([32, 512], F32)
        impT = self.aa([32, 128], F32)
        im1 = self.aa([128, 32], F32)
        im2 = self.aa([128, 32], F32)
        m8 = self.aa([128, 16], F32)
        seln = self.aa([128, 32], F32)
        e0132 = self.aa([32, 128], BF16)
        e01 = e0132[0:1, :]
        tiny32 = self.aa([32, 8], BF16)
        tiny_ = tiny32[0:1, :]
        tiny = tiny_[:, 0:1].to_broadcast([1, 512])
        OTK = ("ar", "ot")
        for b_ in (1, 2):
            P.op("pool", (lambda b_: lambda e: e.memset(VX[b_][:, :, 64:128], 1.0))(b_), writes=[("ar", "vx1", b_)])
        P.op("dve", lambda e: e.memset(e0132[:], 0.0), writes=[("ar", "e01")])
        P.op("dve", lambda e: e.memset(e0132[0:1, 64:128], 1.0), writes=[("ar", "e01")])
        P.op("dve", lambda e: e.memset(tiny32[:], 0.0), writes=[("ar", "e01")])
        P.op("dve", lambda e: e.memset(tiny32[0:1, :], 1e-30), writes=[("ar", "e01")])
        P.op("dve", lambda e: e.memset(GT64[:], 0.0), writes=[("ar", "gt")])
        P.op("dve", lambda e: e.memset(KTc[:], 0.0), writes=[("ar", "ktc")])
        P.op("dve", lambda e: e.memset(VXc[:], 0.0), writes=[("ar", "vxc")])
        P.op("dve", lambda e: e.memset(VXc[:, 64:128], 1.0), writes=[("ar", "vxc")])
        wt, wk = self.W.acquire()
        wv = wt[:, 0:KC * 48].rearrange("p (c n) -> p c n", c=KC)

        def post_gate(tb, b):
            blk = slice(tb * 512, (tb + 1) * 512)
            P.op("act", lambda e: e.activation(out=GT[:, blk], in_=self.ps[b][0:48, :], func=AF.Sigmoid),
                 reads=[("ps", b)], writes=[("ar", "gt")])
        self.proj_fm(wv, wk, 0, 48, post_gate)
        self.W.release()
        order = [0, 2, 1, 3]
        for g in range(4):
            wt, wk = self.W.acquire()
            wv = wt[:, 0:KC * 256].rearrange("p (c n) -> p c n", c=KC)
            for pos, r in enumerate(order):
                def dst_fn(tb, pos=pos):
                    return QT[:, tb * 4:(tb + 1) * 4, pos, :]
                self.proj_fm(wv, wk, r * 64, 64, self.rope_post(dst_fn, ("ar", "qt"), 0.125))
            self.W.release()
            wt, wk = self.W.acquire()
            wv = wt[:, 0:KC * 256].rearrange("p (c n) -> p c n", c=KC)
            for b_ in range(3):
                dstk = KT0 if b_ == 0 else KT[b_]
                dk = OTK if b_ == 0 else ("ar", "kt", b_)
                self.proj_fm(wv, wk, b_ * 64, 64, self.rope_post(
                    (lambda dstk: lambda tb: dstk[:, tb * 512:(tb + 1) * 512].rearrange("p (a b) -> p a b", a=4))(dstk), dk, 1.0))

            def post_v0(tb, b):
                blk = slice(tb * 512, (tb + 1) * 512)
                P.op("act", lambda e: e.copy(out=V0T[:, blk], in_=self.ps[b][0:64, :]), reads=[("ps", b)], writes=[OTK])
            self.proj_fm(wv, wk, 192, 64, post_v0)
            self.W.release()
            wt, wk = self.W.acquire()
            wv = wt[:, 0:KC * 128].rearrange("p (c n) -> p c n", c=KC)
            for b_ in (1, 2):
                self.proj_tm(wv, wk, (b_ - 1) * 64, VX[b_], ("ar", "vx", b_))
            self.W.release()
            w1s = [None] * 4
            for i in range(2):
                wt_, wk_ = self.W.acquire()
                w1s[i] = (wt_[0:64, 0:2048].rearrange("p (l n) -> p l n", l=16), wk_)
            wt, wks = self.W.acquire()
            def comp(which, src, wt, wks, w1s):
                b = self.psb(6, 8)
                srcv = src.rearrange("p (n s) -> p n s", s=16)
                nmm = 64
                cnt = 0
                for l_ in range(32):
                    w1v, w1k = w1s[which * 2 + l_ // 16]
                    rhs = srcv[:, 0:127, l_] if l_ < 16 else srcv[:, 1:128, l_ - 16]
                    pe_col = wt[0:64, 128 + which * 32 + l_:128 + which * 32 + l_ + 1].to_broadcast([64, 127])
                    for rr, rk in ((rhs, [OTK]), (pe_col, [wks])):
                        P.op("pe", (lambda b, w1v, l_, rr, cnt: lambda e: e.matmul(
                            self.ps[b][:, 0:127], lhsT=w1v[:, l_ % 16, :], rhs=rr, start=(cnt == 0), stop=(cnt == nmm - 1)))(b, w1v, l_, rr, cnt),
                            reads=[w1k] + rk, writes=[("ps", b)])
                        cnt += 1
                u, t_, sg_ = gel
                GK = ("ar", "gel")
                P.op("act", lambda e: e.copy(out=u[:, 0:127], in_=self.ps[b][:, 0:127]), reads=[("ps", b)], writes=[GK])
                P.op("dve", lambda e: e.tensor_tensor(out=t_[:, 0:127], in0=u[:, 0:127], in1=u[:, 0:127], op=ALU.mult), reads=[GK], writes=[GK])
                P.op("dve", lambda e: e.tensor_scalar(out=t_[:, 0:127], in0=t_[:, 0:127], scalar1=0.044715, scalar2=1.0,
                                                      op0=ALU.mult, op1=ALU.add), reads=[GK], writes=[GK])
                P.op("dve", lambda e: e.tensor_tensor(out=t_[:, 0:127], in0=t_[:, 0:127], in1=u[:, 0:127], op=ALU.mult), reads=[GK], writes=[GK])
                P.op("act", lambda e: e.activation(out=sg_[:, 0:127], in_=t_[:, 0:127], func=AF.Sigmoid, scale=1.5957691216057308),
                     reads=[GK], writes=[GK])
                P.op("dve", lambda e: e.tensor_tensor(out=gbf[:, 0:127], in0=u[:, 0:127], in1=sg_[:, 0:127], op=ALU.mult),
                     reads=[GK], writes=[("ar", "gbf")])
                b2 = self.psb(6, 8)
                if which == 0:
                    P.op("pe", lambda e: e.matmul(self.ps[b2][0:64, 0:127], lhsT=wt[:, 0:64], rhs=gbf[:, 0:127], start=True, stop=True),
                         reads=[wks, ("ar", "gbf")], writes=[("ps", b2)])
                    P.op("act", lambda e: e.copy(out=KTc[:, 0:127], in_=self.ps[b2][0:64, 0:127]), reads=[("ps", b2)], writes=[("ar", "ktc")])
                else:
                    P.op("pe", lambda e: e.matmul(self.ps[b2][0:127, 0:64], lhsT=gbf[:, 0:127], rhs=wt[:, 64:128], start=True, stop=True),
                         reads=[wks, ("ar", "gbf")], writes=[("ps", b2)])
                    P.op("act", lambda e: e.copy(out=VXc[0:127, 0:64], in_=self.ps[b2][0:127, 0:64]), reads=[("ps", b2)], writes=[("ar", "vxc")])
            comp(0, KT0, wt, wks, list(w1s))
            self.W.release()
            self.W.release()
            for i in range(2, 4):
                wt_, wk_ = self.W.acquire()
                w1s[i] = (wt_[0:64, 0:2048].rearrange("p (l n) -> p l n", l=16), wk_)
            comp(1, V0T, wt, wks, list(w1s))
            self.W.release()
            self.W.release()
            self.W.release()
            SB = (0, 1, 2)
            rd_s = rd
            rd_br = {2: rg, 1: rd2}
            accs = [acc, acc2]
            accc = [self.rsc[0][0], self.rsc[0][1]]
            accck = [("ar", "qf", 0), ("ar", "t1", 0)]

            def bc4(ap):
                return ap.rearrange("p (o n) -> p o n", o=1).to_broadcast([ap.shape[0], 4, 128])

            def gates_ps(br, qt, g=g):
                tsl = slice(qt * 128, (qt + 1) * 128)
                bg = self.psb(6, 8)
                for pos, r in enumerate(order):
                    row = br * 16 + g * 4 + r
                    P.op("pe", (lambda bg, pos, row: lambda e: e.matmul(
                        self.ps[bg][0:64, pos * 128:(pos + 1) * 128], lhsT=self.identb[0:48, row:row + 1].to_broadcast([48, 64]),
                        rhs=GT[:, tsl], start=True, stop=True))(bg, pos, row),
                        reads=["identb", ("ar", "gt")], writes=[("ps", bg)])
                return bg

            def S1(qt):
                par = qt % 2
                tsl = slice(qt * 128, (qt + 1) * 128)
                qrhs = QT[:, qt, :, :]
                steps = [(KTc[:], [("ar", "ktc")], [(self.identb[:], bc4(mw[:, tsl]), ["identb", ("ar", "mw")])],
                          VXc[:], [("ar", "vxc")])]
                pi = self.ptc % 3
                bo_c = self.attn(qrhs, [("ar", "qt")], steps, den_extra=(e01, tiny, [("ar", "e01")]), bo=3, sbanks=SB)
                bi = self.psb(6, 8)
                ptile = self.pt[pi]
                P.op("pe", lambda e: e.matmul(self.ps[bi][0:32, :], lhsT=ov[:], rhs=ptile[:], start=True, stop=True),
                     reads=[("ar", "ov"), ("ar", "pt", pi)], writes=[("ps", bi)])
                P.op("act", lambda e: e.activation(out=rd_s[:], in_=self.ps[bo_c][64:128, :], func=AF.Ln), reads=[("ps", bo_c)], writes=[("ar", "qf", 1)])
                P.op("act", lambda e: e.activation(out=rd_s[:], in_=rd_s[:], func=AF.Exp, scale=-1.0), reads=[("ar", "qf", 1)], writes=[("ar", "qf", 1)])
                P.op("dve", lambda e: e.tensor_tensor(out=impn[:], in0=self.ps[bi][0:32, :], in1=rd_s[0:32, :], op=ALU.mult),
                     reads=[("ps", bi), ("ar", "qf", 1)], writes=[("ar", "impn")])
                P.op("dve", lambda e: e.tensor_tensor(out=impT[:], in0=impn[:, 0:128], in1=impn[:, 128:256], op=ALU.add),
                     reads=[("ar", "impn")], writes=[("ar", "impT")])
                P.op("dve", lambda e: e.tensor_tensor(out=impT[:], in0=impT[:], in1=impn[:, 256:384], op=ALU.add),
                     reads=[("ar", "impn"), ("ar", "impT")], writes=[("ar", "impT")])
                P.op("dve", lambda e: e.tensor_tensor(out=impT[:], in0=impT[:], in1=impn[:, 384:512], op=ALU.add),
                     reads=[("ar", "impn"), ("ar", "impT")], writes=[("ar", "impT")])
                bg = gates_ps(0, qt)
                ac = accc[par]
                ak = accck[par]
                P.op("dve", lambda e: e.tensor_tensor(out=ac[:], in0=self.ps[bg][0:64, :], in1=rd_s[:], op=ALU.mult),
                     reads=[("ps", bg), ("ar", "qf", 1)], writes=[ak])
                P.op("dve", lambda e: e.tensor_tensor(out=ac[:], in0=self.ps[bo_c][0:64, :], in1=ac[:], op=ALU.mult),
                     reads=[("ps", bo_c), ak], writes=[ak])

            def S2(qt):
                bt = self.psb(6, 8)
                P.op("pe", lambda e: e.matmul(self.ps[bt][:, 0:32], lhsT=impT[:], rhs=self.ident[0:32, 0:32], start=True, stop=True),
                     reads=[("ar", "impT"), "ident"], writes=[("ps", bt)])
                fsl = slice(32 - 2 * qt, 64 - 2 * qt)
                P.op("dve", lambda e: e.tensor_tensor(out=im1[:], in0=self.ps[bt][:, 0:32], in1=kw[:, fsl], op=ALU.mult),
                     reads=[("ps", bt), ("ar", "fw")], writes=[("ar", "im1")])
                P.op("dve", lambda e: e.tensor_tensor(out=im1[:], in0=im1[:], in1=fw[:, fsl], op=ALU.add),
                     reads=[("ar", "im1"), ("ar", "fw")], writes=[("ar", "im1")])
                P.op("dve", lambda e: e.memset(im1[:, 0:1], 1.0e4), reads=[("ar", "im1")], writes=[("ar", "im1")])
                P.op("dve", lambda e: e.max(out=m8[:, 0:8], in_=im1[:]), reads=[("ar", "im1")], writes=[("ar", "m8")])
                P.op("dve", lambda e: e.match_replace(out=im2[:], in_to_replace=m8[:, 0:8], in_values=im1[:], imm_value=-1.0e30),
                     reads=[("ar", "im1"), ("ar", "m8")], writes=[("ar", "im2")])
                P.op("dve", lambda e: e.max(out=m8[:, 8:16], in_=im2[:]), reads=[("ar", "im2")], writes=[("ar", "m8b")])
                P.op("dve", lambda e: e.tensor_scalar(out=seln[:], in0=im1[:], scalar1=m8[:, 15:16], scalar2=None, op0=ALU.is_ge),
                     reads=[("ar", "im1"), ("ar", "m8b")], writes=[("ar", "seln")])
                P.op("dve", lambda e: e.tensor_scalar(out=seln[:], in0=seln[:], scalar1=1.0, scalar2=-NEGM, op0=ALU.subtract, op1=ALU.mult),
                     reads=[("ar", "seln")], writes=[("ar", "seln")])

            def S3(qt):
                par = qt % 2
                bt2 = self.psb(6, 8)
                P.op("pe", lambda e: e.matmul(self.ps[bt2][0:32, 0:128], lhsT=seln[:], rhs=self.ident[:], start=True, stop=True),
                     reads=[("ar", "seln"), "ident"], writes=[("ps", bt2)])
                P.op("act", lambda e: e.copy(out=QTX[64:96, qt, :, :], in_=bc4(self.ps[bt2][0:32, 0:128])),
                     reads=[("ps", bt2)], writes=[("ar", "qsel", par)])

            def fin(bo, br, qt, first):
                par = qt % 2
                rd_m = rd_br[br]
                rmk = ("ar", "t1", 1) if br == 2 else ("ar", "rd_m", br)
                acc = accs[par]
                acck = ("ar", "t2", 1) if par == 0 else ("ar", "acc", par)
                P.op("act", lambda e: e.activation(out=rd_m[:], in_=self.ps[bo][64:128, :], func=AF.Ln), reads=[("ps", bo)], writes=[rmk])
                P.op("act", lambda e: e.activation(out=rd_m[:], in_=rd_m[:], func=AF.Exp, scale=-1.0), reads=[rmk], writes=[rmk])
                bg = gates_ps(br, qt)
                P.op("dve", lambda e: e.tensor_tensor(out=rd_m[:], in0=self.ps[bg][0:64, :], in1=rd_m[:], op=ALU.mult),
                     reads=[("ps", bg), rmk], writes=[rmk])
                P.op("dve", lambda e: e.tensor_tensor(out=rd_m[:], in0=self.ps[bo][0:64, :], in1=rd_m[:], op=ALU.mult),
                     reads=[("ps", bo), rmk], writes=[rmk])
                src = accc[par] if first else acc
                sk = accck[par] if first else acck
                P.op("dve", lambda e: e.tensor_tensor(out=acc[:], in0=src[:], in1=rd_m[:], op=ALU.add),
                     reads=[sk, rmk], writes=[acck])

            def Mw(qt):
                qrhs = QT[:, qt, :, :]
                steps = []
                for kt in range(max(0, qt - 4), qt + 1):
                    masks = []
                    if kt == qt - 4:
                        masks.append((self.identb[:], bc4(bw[:]), ["identb", ("ar", "bw")]))
                    if kt == qt:
                        masks.append((self.identb[:], bc4(cw[:, 384:512]), ["identb", ("ar", "cw")]))
                    steps.append((KT[2][:, kt * 128:(kt + 1) * 128], [("ar", "kt", 2)], masks, VX[2][:, kt, :], [("ar", "vx", 2), ("ar", "vx1", 2)]))
                bo_w = self.attn(qrhs, [("ar", "qt")], steps, bo=4, sbanks=SB)
                fin(bo_w, 2, qt, True)

            def Ms_attn(qt):
                par = qt % 2
                qrhs = QTX[:, qt, :, :]
                steps = []
                for kt in range(0, qt + 1):
                    masks = []
                    if kt == qt:
                        masks.append((self.identb[:], bc4(cw[:, 384:512]), ["identb", ("ar", "cw")]))
                    steps.append((KT1X[:, kt * 128:(kt + 1) * 128], [("ar", "kt", 1), ("ar", "ew")], masks, VX[1][:, kt, :],
                                  [("ar", "vx", 1), ("ar", "vx1", 1)]))
                return self.attn(qrhs, [("ar", "qt"), ("ar", "qsel", par)], steps, bo=5, sbanks=SB)

            def Ms_fin(bo_s, qt):
                tsl = slice(qt * 128, (qt + 1) * 128)
                fin(bo_s, 1, qt, False)
                accq = accs[qt % 2]
                aqk = ("ar", "t2", 1) if qt % 2 == 0 else ("ar", "acc", qt % 2)
                P.op("act", lambda e: e.copy(out=OT[0:64, :, tsl], in_=accq[:, 0:256].rearrange("p (a b) -> p a b", a=2)),
                     reads=[aqk], writes=[OTK])
                P.op("act", lambda e: e.copy(out=OT[64:128, :, tsl], in_=accq[:, 256:512].rearrange("p (a b) -> p a b", a=2)),
                     reads=[aqk], writes=[OTK])

            S1(0)
            S2(0)
            S3(0)
            for qt in range(NT):
                if qt + 1 < NT:
                    S1(qt + 1)
                Mw(qt)
                if qt + 1 < NT:
                    S2(qt + 1)
                bo_s = Ms_attn(qt)
                if qt + 1 < NT:
                    S3(qt + 1)
                Ms_fin(bo_s, qt)
            wt, wk = self.W.acquire()
            wv = wt[:, 0:2048].rearrange("p (a n) -> p a n", a=2)
            self.out_proj(wv, wk, 2, OT, OTK)
            self.W.release()
        self.dump("ot", OT[:], [OTK], BF16)
        self.dump("ktc", KTc[:], [("ar", "ktc")], BF16)
        self.dump("vxc", VXc[:], [("ar", "vxc")], BF16)
        self.dump("gt", GT[:], [("ar", "gt")], BF16)
        self.dump("seln", seln[:], [("ar", "seln")])
        self.dump("im1", im1[:], [("ar", "im1")])
        self.dump("m8", m8[:], [("ar", "m8"), ("ar", "m8b")])

def consts():
    import ml_dtypes
    bf = ml_dtypes.bfloat16
    c = {"c_ident": np.eye(128, dtype=np.float32)}
    inv = (10000.0 ** (-np.arange(0, 64, 2, dtype=np.float32) / 64)).astype(np.float32)
    ang = np.arange(S, dtype=np.float32)[:, None] * inv[None, :]
    cos = np.cos(ang).astype(np.float32).T
    sin = np.sin(ang).astype(np.float32).T
    c["c_cos"] = np.ascontiguousarray(np.concatenate([cos, cos], 0))
    c["c_sin"] = np.ascontiguousarray(np.concatenate([sin, sin], 0))
    rot = np.zeros((64, 64), np.float32)
    for d in range(32):
        rot[d + 32, d] = -1.0
        rot[d, d + 32] = 1.0
    c["c_rot"] = rot
    sp = np.arange(128)[:, None]
    cc = np.arange(896)[None, :]
    c["c_cw"] = np.where(sp + 384 <= cc, 0.0, NEGM).astype(bf)
    ii = np.arange(128)[None, :]
    c["c_bw"] = np.where(sp > ii, 0.0, NEGM).astype(bf)
    n = np.arange(128)[:, None]
    t = np.arange(S)[None, :]
    c["c_mw"] = np.where((n * 16 + 31 <= t) & (n < 127), 0.0, NEGM).astype(bf)
    jj = np.arange(32)[:, None]
    c["c_ew"] = (jj == (np.arange(S)[None, :] // 64)).astype(np.float32).astype(bf)
    fw = np.zeros((128, 64), np.float32)
    kw = np.ones((128, 64), np.float32)
    for i in range(128):
        tbr = 1 if i >= 64 else 0
        for cidx in range(64):
            rel = cidx - 32
            if rel == tbr:
                fw[i, cidx] = 2.0e4; kw[i, cidx] = 0.0
            elif rel == tbr - 1:
                fw[i, cidx] = 3.0e4; kw[i, cidx] = 0.0
            elif rel > tbr:
                fw[i, cidx] = -1.0e4 * (cidx + 1); kw[i, cidx] = 0.0
    c["c_fw"] = fw
    c["c_kw"] = kw
    cs = np.arange(127) * 16
    ce = cs + 32
    ss = np.arange(32) * 64
    se = ss + 64
    ovm = np.clip(np.minimum(ce[:, None], se[None, :]) - np.maximum(cs[:, None], ss[None, :]), 0, None) / 32.0
    ovp = np.zeros((128, 32), np.float32)
    ovp[:127] = ovm
    c["c_ov"] = ovp.astype(bf)
    return c


_CACHE = {}


def kernel(**inputs):
    cfg = {}
    b = Builder(cfg)
    nc = b.build()
    names = list(b.dram.keys())
    cs = consts()
    in_maps = []
    x = np.ascontiguousarray(inputs["x"])
    for core in range(8):
        m = {}
        for nm in names:
            if nm == "x":
                m[nm] = x[core]
            elif nm in cs:
                m[nm] = cs[nm]
            else:
                m[nm] = np.ascontiguousarray(inputs[nm])
        in_maps.append(m)
    res = run_bass_kernel_spmd(nc, in_maps, core_ids=list(range(8)))
    return np.stack([r["out"] for r in res.results], axis=0)
```
